# Optimizing a Trainium2 kernel written in Bass

```python
import math
import jax, jax.numpy as jnp
from jax import lax
import numpy as np

D_MODEL = 1024
BATCH = 8
SEQ = 8192
DEPTH = 2

HEAD_DIM = 64
M_HEADS = 4
M_DIM = 64
CONV_WIDTH = 4
CHUNK = 128
S_HEADS = 8
S_KV_HEADS = 2
WINDOW = 128
D_HEADS = 4
D_QK = 32
D_V = 64
Q_BLOCK = 128
ROPE_THETA = 10000.0
D_FF = 4 * D_MODEL
EPS = 1e-6

M_WIDTH = M_HEADS * M_DIM
S_WIDTH = S_HEADS * HEAD_DIM
S_KV_WIDTH = S_KV_HEADS * HEAD_DIM
D_QK_WIDTH = D_HEADS * 2 * D_QK
D_WIDTH = D_HEADS * D_V
MIX_WIDTH = M_WIDTH + S_WIDTH + D_WIDTH
SPLIT_SIZES = (M_WIDTH, M_WIDTH, M_WIDTH, M_WIDTH, M_HEADS, M_HEADS,
               S_WIDTH, S_KV_WIDTH, S_KV_WIDTH,
               D_QK_WIDTH, D_QK_WIDTH, D_WIDTH)
IN_WIDTH = sum(SPLIT_SIZES)

kernel_name = 'hybrid_mlstm_swa_diffattn_block'


def rms_norm(x, g):
    xf = x.astype(jnp.float32)
    y = xf * lax.rsqrt(jnp.mean(xf * xf, axis=-1, keepdims=True) + EPS)
    return (y * g.astype(jnp.float32)).astype(x.dtype)


def rope(x, pos):
    d = x.shape[-1]
    half = d // 2
    inv = ROPE_THETA ** (-jnp.arange(half, dtype=jnp.float32) * 2.0 / d)
    ang = pos.astype(jnp.float32)[:, None] * inv[None, :]
    cos = jnp.cos(ang)[None, :, None, :]
    sin = jnp.sin(ang)[None, :, None, :]
    xf = x.astype(jnp.float32)
    x1, x2 = xf[..., :half], xf[..., half:]
    return jnp.concatenate([x1 * cos - x2 * sin, x2 * cos + x1 * sin], axis=-1).astype(x.dtype)


def split_columns(z):
    parts = []
    start = 0
    for size in SPLIT_SIZES:
        parts.append(z[..., start:start + size])
        start += size
    return parts


def causal_conv(x, w, b):
    K = w.shape[0]
    S = x.shape[1]
    xp = jnp.pad(x, ((0, 0), (K - 1, 0), (0, 0)))
    y = xp[:, 0:S, :] * w[0]
    for j in range(1, K):
        y = y + xp[:, j:j + S, :] * w[j]
    return y + b


def mlstm_chunkwise(q, k, v, i_pre, f_pre):
    B, S, H, D = q.shape
    L = CHUNK
    nC = S // L
    qf = q.astype(jnp.float32)
    kf = k.astype(jnp.float32) * (D ** -0.5)
    vf = v.astype(jnp.float32)
    logf = jax.nn.log_sigmoid(f_pre.astype(jnp.float32))
    ig = i_pre.astype(jnp.float32)

    def to_chunks(a):
        a = a.reshape((B, nC, L, H) + a.shape[3:])
        return jnp.moveaxis(a, (1, 3), (0, 2))

    causal = jnp.tril(jnp.ones((L, L), dtype=bool))

    def step(carry, xs):
        C, n, m = carry
        qc, kc, vc, ic, fc = xs
        b = jnp.cumsum(fc, axis=-1)
        dmat = b[..., :, None] - b[..., None, :] + ic[..., None, :]
        dmat = jnp.where(causal, dmat, -jnp.inf)
        inter = b + m[..., None]
        m_t = jnp.maximum(inter, jnp.max(dmat, axis=-1))
        w_intra = jnp.exp(dmat - m_t[..., None])
        sc = jnp.einsum('bhtd,bhsd->bhts', qc, kc) * w_intra
        a_inter = jnp.exp(inter - m_t)
        num = (a_inter[..., None] * jnp.einsum('bhvk,bhtk->bhtv', C, qc)
               + jnp.einsum('bhts,bhsv->bhtv', sc, vc))
        den = a_inter * jnp.einsum('bhk,bhtk->bht', n, qc) + jnp.sum(sc, axis=-1)
        h = num / jnp.maximum(jnp.abs(den), jnp.exp(-m_t))[..., None]
        b_last = b[..., -1]
        g = b_last[..., None] - b + ic
        m_new = jnp.maximum(b_last + m, jnp.max(g, axis=-1))
        w_state = jnp.exp(g - m_new[..., None])
        decay = jnp.exp(b_last + m - m_new)
        C_new = decay[..., None, None] * C + jnp.einsum('bhs,bhsv,bhsk->bhvk', w_state, vc, kc)
        n_new = decay[..., None] * n + jnp.einsum('bhs,bhsk->bhk', w_state, kc)
        return (C_new, n_new, m_new), h

    init = (jnp.zeros((B, H, D, D), jnp.float32),
            jnp.zeros((B, H, D), jnp.float32),
            jnp.zeros((B, H), jnp.float32))
    xs = (to_chunks(qf), to_chunks(kf), to_chunks(vf), to_chunks(ig), to_chunks(logf))
    _, h = lax.scan(step, init, xs)
    return jnp.moveaxis(h, (0, 2), (1, 3)).reshape(B, S, H, D)


def sliding_window_gqa(q, k, v, sinks):
    B, S, Hq, D = q.shape
    Hkv = k.shape[2]
    G = Hq // Hkv
    W = WINDOW
    nB = S // W
    qb = q.reshape(B, nB, W, Hkv, G, D)

    def band(a):
        ab = a.reshape(B, nB, W, Hkv, D)
        prev = jnp.pad(ab, ((0, 0), (1, 0), (0, 0), (0, 0), (0, 0)))[:, :-1]
        return jnp.concatenate([prev, ab], axis=2)

    kb, vb = band(k), band(v)
    s = jnp.einsum('bnqhgd,bnkhd->bnhgqk', qb, kb).astype(jnp.float32) * (D ** -0.5)
    qpos = jnp.arange(W)[:, None] + W
    kpos = jnp.arange(2 * W)[None, :]
    rel = qpos - kpos
    valid = (rel >= 0) & (rel < W)
    valid = valid[None] & ((jnp.arange(nB)[:, None, None] > 0) | (kpos[None] >= W))
    s = jnp.where(valid[None, :, None, None], s, -jnp.inf)
    sink = sinks.astype(jnp.float32).reshape(Hkv, G)[None, None, :, :, None]
    mx = jnp.maximum(jnp.max(s, axis=-1), sink)
    e = jnp.exp(s - mx[..., None])
    denom = jnp.sum(e, axis=-1) + jnp.exp(sink - mx)
    p = e / denom[..., None]
    o = jnp.einsum('bnhgqk,bnkhd->bnqhgd', p.astype(v.dtype), vb)
    return o.reshape(B, S, Hq, D)


def differential_attention(q, k, v, lam):
    B, S, H, _, Dk = q.shape
    nQ = S // Q_BLOCK
    qb = jnp.moveaxis(q.reshape(B, nQ, Q_BLOCK, H, 2, Dk), 1, 0)
    kpos = jnp.arange(S)

    def block(args):
        qblk, i = args
        s = jnp.einsum('bqhmd,bkhmd->bhmqk', qblk, k).astype(jnp.float32) * (Dk ** -0.5)
        qpos = i * Q_BLOCK + jnp.arange(Q_BLOCK)
        s = jnp.where(kpos[None, :] <= qpos[:, None], s, -jnp.inf)
        p = jax.nn.softmax(s, axis=-1)
        a = p[:, :, 0] - lam * p[:, :, 1]
        return jnp.einsum('bhqk,bkhd->bqhd', a.astype(v.dtype), v)

    o = lax.map(block, (qb, jnp.arange(nQ)))
    return jnp.moveaxis(o, 0, 1).reshape(B, S, H, v.shape[-1])


def hybrid_layer(x, layer_idx, w_in, conv_w, conv_b, i_bias, f_bias, m_norm_g, sinks,
                 lam_q1, lam_k1, lam_q2, lam_k2, sub_g, w_out, w_up, w_down,
                 g_pre_mix, g_post_mix, g_pre_mlp, g_post_mlp):
    B, S, _ = x.shape
    pos = jnp.arange(S)
    h = rms_norm(x, g_pre_mix)
    z = h @ w_in
    mq, mk, mv, mo, mi, mf, sq, sk, sv, dq, dk, dv = split_columns(z)

    qk = jax.nn.silu(causal_conv(jnp.concatenate([mq, mk], axis=-1), conv_w, conv_b))
    m_q = qk[..., :M_WIDTH].reshape(B, S, M_HEADS, M_DIM)
    m_k = qk[..., M_WIDTH:].reshape(B, S, M_HEADS, M_DIM)
    m_v = mv.reshape(B, S, M_HEADS, M_DIM)
    h_m = mlstm_chunkwise(m_q, m_k, m_v, mi + i_bias, mf + f_bias)
    h_m = rms_norm(h_m, m_norm_g.reshape(M_HEADS, M_DIM)).reshape(B, S, M_WIDTH)
    out_m = (jax.nn.sigmoid(mo.astype(jnp.float32)) * h_m).astype(x.dtype)

    s_q = rope(sq.reshape(B, S, S_HEADS, HEAD_DIM), pos)
    s_k = rope(sk.reshape(B, S, S_KV_HEADS, HEAD_DIM), pos)
    s_v = sv.reshape(B, S, S_KV_HEADS, HEAD_DIM)
    out_s = sliding_window_gqa(s_q, s_k, s_v, sinks).reshape(B, S, S_WIDTH)

    d_q = rope(dq.reshape(B, S, D_HEADS * 2, D_QK), pos).reshape(B, S, D_HEADS, 2, D_QK)
    d_k = rope(dk.reshape(B, S, D_HEADS * 2, D_QK), pos).reshape(B, S, D_HEADS, 2, D_QK)
    d_v = dv.reshape(B, S, D_HEADS, D_V)
    lam_init = 0.8 - 0.6 * math.exp(-0.3 * layer_idx)
    lam = (jnp.exp(jnp.sum(lam_q1.astype(jnp.float32) * lam_k1.astype(jnp.float32)))
           - jnp.exp(jnp.sum(lam_q2.astype(jnp.float32) * lam_k2.astype(jnp.float32)))
           + lam_init)
    o_d = differential_attention(d_q, d_k, d_v, lam)
    out_d = (rms_norm(o_d, sub_g) * (1.0 - lam_init)).reshape(B, S, D_WIDTH).astype(x.dtype)

    mix = jnp.concatenate([out_m, out_s.astype(x.dtype), out_d], axis=-1) @ w_out
    x = x + rms_norm(mix, g_post_mix)

    h2 = rms_norm(x, g_pre_mlp)
    u = jnp.square(jax.nn.relu(h2 @ w_up))
    x = x + rms_norm(u @ w_down, g_post_mlp)
    return x


def setup_inputs(seed: int = 0) -> dict:
    key = jax.random.key(seed)
    ks = jax.random.split(key, 21)
    f32 = jnp.float32

    def nrm(k, shape, scale):
        return jax.random.normal(k, shape, f32) * scale

    def gain(k, shape):
        return 1.0 + 0.02 * jax.random.normal(k, shape, f32)

    return {
        'x': jax.random.normal(ks[0], (BATCH, SEQ, D_MODEL), f32),
        'w_in': nrm(ks[1], (DEPTH, D_MODEL, IN_WIDTH), D_MODEL ** -0.5),
        'conv_w': nrm(ks[2], (DEPTH, CONV_WIDTH, 2 * M_WIDTH), CONV_WIDTH ** -0.5),
        'conv_b': nrm(ks[3], (DEPTH, 2 * M_WIDTH), 0.01),
        'i_bias': nrm(ks[4], (DEPTH, M_HEADS), 0.1),
        'f_bias': 3.0 + nrm(ks[5], (DEPTH, M_HEADS), 0.5),
        'm_norm_g': gain(ks[6], (DEPTH, M_WIDTH)),
        'sinks': nrm(ks[7], (DEPTH, S_HEADS), 0.5),
        'lam_q1': nrm(ks[8], (DEPTH, D_QK), 0.1),
        'lam_k1': nrm(ks[9], (DEPTH, D_QK), 0.1),
        'lam_q2': nrm(ks[10], (DEPTH, D_QK), 0.1),
        'lam_k2': nrm(ks[11], (DEPTH, D_QK), 0.1),
        'sub_g': gain(ks[12], (DEPTH, D_V)),
        'w_out': nrm(ks[13], (DEPTH, MIX_WIDTH, D_MODEL), MIX_WIDTH ** -0.5),
        'w_up': nrm(ks[14], (DEPTH, D_MODEL, D_FF), D_MODEL ** -0.5),
        'w_down': nrm(ks[15], (DEPTH, D_FF, D_MODEL), D_FF ** -0.5),
        'g_pre_mix': gain(ks[16], (DEPTH, D_MODEL)),
        'g_post_mix': gain(ks[17], (DEPTH, D_MODEL)),
        'g_pre_mlp': gain(ks[18], (DEPTH, D_MODEL)),
        'g_post_mlp': gain(ks[19], (DEPTH, D_MODEL)),
    }


def reference(x, w_in, conv_w, conv_b, i_bias, f_bias, m_norm_g, sinks,
              lam_q1, lam_k1, lam_q2, lam_k2, sub_g, w_out, w_up, w_down,
              g_pre_mix, g_post_mix, g_pre_mlp, g_post_mlp):
    for l in range(DEPTH):
        x = hybrid_layer(x, l, w_in[l], conv_w[l], conv_b[l], i_bias[l], f_bias[l],
                         m_norm_g[l], sinks[l], lam_q1[l], lam_k1[l], lam_q2[l], lam_k2[l],
                         sub_g[l], w_out[l], w_up[l], w_down[l],
                         g_pre_mix[l], g_post_mix[l], g_pre_mlp[l], g_post_mlp[l])
    return x
```

```python
import math
import numpy as np
import concourse.bass as bass
import concourse.mybir as mybir
from concourse.bass_utils import run_bass_kernel_spmd
from contextlib import ExitStack

F32 = mybir.dt.float32
BF16 = mybir.dt.bfloat16
AF = mybir.ActivationFunctionType
ALU = mybir.AluOpType

D = 1024
SEQ = 8192
NB = 8
DEPTH = 2
DFF = 4096
INW = 2568
EPS = 1e-6
class Buf:
    __slots__ = ("name", "w", "r")

    def __init__(self, name):
        self.name = name
        self.w = None
        self.r = []


class Ins:
    __slots__ = ("eng", "fn", "deps", "sig", "cnt", "dma", "sem", "semval", "idx", "raw", "pos")

    def __init__(self, eng, fn, dma=False):
        self.eng = eng
        self.fn = fn
        self.deps = []
        self.sig = False
        self.cnt = 0
        self.dma = dma
        self.sem = None
        self.semval = 0
        self.idx = 0
        self.raw = ()
        self.pos = 0


class Prog:
    ENGS = ("tensor", "vector", "scalar", "gpsimd", "sync")

    def __init__(self, nc, paranoid=False):
        self.nc = nc
        self.paranoid = paranoid
        self.q = {e: [] for e in self.ENGS}
        self.dma_keys = {}
        self.all = []

    SAME_ENG_WINDOW = 4

    def _same_eng_sync(self, ins, d):
        if d.eng != ins.eng or d.dma or ins.eng == "tensor" or ins.eng == "sync":
            return False
        return (d in ins.raw) and (ins.pos - d.pos) <= self.SAME_ENG_WINDOW

    def _track(self, ins, reads, writes):
        deps = set()
        for b in reads:
            if b.w is not None:
                deps.add(b.w)
        ins.raw = frozenset(deps)
        for b in writes:
            if b.w is not None:
                deps.add(b.w)
            for r in b.r:
                deps.add(r)
        deps.discard(ins)
        ins.deps = list(deps)
        for b in reads:
            b.r.append(ins)
        for b in writes:
            b.w = ins
            b.r = []

    def op(self, eng, fn, reads=(), writes=()):
        ins = Ins(eng, fn)
        ins.idx = len(self.all)
        ins.pos = len(self.q[eng])
        self.all.append(ins)
        self._track(ins, reads, writes)
        self.q[eng].append(ins)
        return ins

    def dma(self, eng, fn, reads=(), writes=(), key=None):
        ins = Ins(eng, fn, dma=True)
        ins.idx = len(self.all)
        ins.pos = len(self.q[eng])
        self.all.append(ins)
        self._track(ins, reads, writes)
        st = self.dma_keys.setdefault(key, [0])
        st[0] += 16
        ins.sem = key
        ins.semval = st[0]
        self.q[eng].append(ins)
        return ins

    def emit(self, final_waits=()):
        nc = self.nc
        for ins in self.all:
            for d in ins.deps:
                if d.dma:
                    continue
                if d.eng == ins.eng and not self._same_eng_sync(ins, d):
                    continue
                d.sig = True
        for e in self.ENGS:
            c = 0
            for ins in self.q[e]:
                if ins.dma:
                    continue
                if ins.sig:
                    c += 1
                ins.cnt = c
        with ExitStack() as es:
            esem = {e: es.enter_context(nc.semaphore("es_" + e)) for e in self.ENGS}
            dsem = {k: es.enter_context(nc.semaphore("ds_%d" % i)) for i, k in enumerate(self.dma_keys)}
            block = es.enter_context(nc.Block())
            stats = {}

            def run(e, h):
                waited = {}
                nw = 0
                for ins in self.q[e]:
                    need = {}
                    for d in ins.deps:
                        if d.dma:
                            s, v = dsem[d.sem], d.semval
                        else:
                            if d.eng == e and not self._same_eng_sync(ins, d):
                                continue
                            s, v = esem[d.eng], d.cnt
                        if waited.get(id(s), 0) >= v:
                            continue
                        if need.get(id(s), (None, 0))[1] < v:
                            need[id(s)] = (s, v)
                    for sid, (s, v) in need.items():
                        h.wait_ge(s, v)
                        waited[sid] = v
                        nw += 1
                    r = ins.fn(h)
                    if ins.dma:
                        r.then_inc(dsem[ins.sem], 16)
                    elif ins.sig:
                        r.then_inc(esem[e], 1)
                for (kind, key_or_ins) in (final_waits if e == "sync" else ()):
                    if kind == "dma":
                        h.wait_ge(dsem[key_or_ins], self.dma_keys[key_or_ins][0])
                stats[e] = (len(self.q[e]), nw)

            @block.tensor
            def _(h):
                run("tensor", h)

            @block.vector
            def _(h):
                run("vector", h)

            @block.scalar
            def _(h):
                run("scalar", h)

            @block.gpsimd
            def _(h):
                run("gpsimd", h)

            @block.sync
            def _(h):
                run("sync", h)
        self.stats = stats
        return stats

NPAR = 208
C_ONES, C_IDENT, C_TRIU, C_NEG, C_P64, C_P32, C_MSC, C_MSP = [i * 128 for i in range(8)]
C_MD = 1024
C_RM = 1024 + 2048
C_HM = C_RM + 4
NCST = C_HM + 4

O_MQ, O_MK, O_MV, O_MO, O_MI, O_MF, O_SQ, O_SK, O_SV, O_DQ, O_DK, O_DV = 0, 256, 512, 768, 1024, 1028, 1032, 1544, 1672, 1800, 2056, 2312
FM_SQ, FM_SK, FM_MO, TMB0, NWB = 512, 1024, 1152, 1408, 1800
A_DQ, A_DK, A_DV, NWA = 0, 256, 512, 768


def _perm_in():
    cols = []
    cols += list(range(O_MQ, O_MQ + 256)) + list(range(O_MK, O_MK + 256))
    for c in range(4):
        cols += list(range(O_SQ + c * 64, O_SQ + c * 64 + 64)) + list(range(O_SQ + (4 + c) * 64, O_SQ + (4 + c) * 64 + 64))
    cols += list(range(O_SK, O_SK + 128))
    cols += list(range(O_MO, O_MO + 256))
    cols += list(range(O_MV, O_MV + 256))
    cols += list(range(O_SV, O_SV + 128)) + list(range(O_MI, O_MI + 4)) + list(range(O_MF, O_MF + 4))
    cols += list(range(O_DQ, O_DQ + 256)) + list(range(O_DK, O_DK + 256)) + list(range(O_DV, O_DV + 256))
    assert len(cols) == INW and len(set(cols)) == INW
    return np.array(cols)


def _consts(S):
    c = np.zeros((128, NCST), np.float32)
    idx = np.arange(128)
    c[:, C_ONES:C_ONES + 128] = 1.0
    c[:, C_IDENT:C_IDENT + 128] = np.eye(128, dtype=np.float32)
    c[:, C_TRIU:C_TRIU + 128] = (idx[:, None] <= idx[None, :])
    c[:, C_NEG:C_NEG + 128] = np.where(idx[:, None] <= idx[None, :], 0.0, -30000.0)
    for (off, hd) in ((C_P64, 64), (C_P32, 32)):
        half = hd // 2
        d = idx % hd
        partner = np.where(d < half, idx + half, idx - half)
        pm = np.zeros((128, 128), np.float32)
        pm[partner, idx] = 1.0
        c[:, off:off + 128] = pm
    c[:, C_MSC:C_MSC + 128] = (idx[:, None] <= idx[None, :])
    c[:, C_MSP:C_MSP + 128] = (idx[:, None] > idx[None, :])
    q = np.arange(512)
    for o in range(4):
        c[:, C_MD + o * 512:C_MD + (o + 1) * 512] = ((o * 128 + idx)[:, None] <= q[None, :])
    for j in range(4):
        c[32 * j:32 * j + 32, C_RM + j] = 1.0
    c[0:64, C_HM] = 1.0
    c[64:128, C_HM + 1] = 1.0
    tabs = np.zeros((4, 128, S), np.float32)
    pos = np.arange(S).astype(np.float32)
    for ti, hd in ((0, 64), (2, 32)):
        half = hd // 2
        inv = (np.float32(10000.0) ** (-np.arange(half, dtype=np.float32) * np.float32(2.0) / np.float32(hd))).astype(np.float32)
        ang = (pos[:, None] * inv[None, :]).astype(np.float32)
        d = idx % hd
        fi = d % half
        sign = np.where(d < half, -1.0, 1.0).astype(np.float32)
        tabs[ti] = np.cos(ang)[:, fi].T
        tabs[ti + 1] = (np.sin(ang)[:, fi] * sign[None, :]).T
    return c, tabs


def _params(inp, l):
    p = np.zeros((128, NPAR), np.float32)
    fm = lambda v: np.asarray(v, np.float32).reshape(-1, 128).T
    p[:, 0:8] = fm(inp['g_pre_mix'][l])
    p[:, 8:16] = fm(inp['g_post_mix'][l])
    p[:, 16:24] = fm(inp['g_pre_mlp'][l])
    p[:, 24:32] = fm(inp['g_post_mlp'][l])
    cw = np.asarray(inp['conv_w'][l], np.float32)
    p[:, 32:48] = cw.reshape(4, 4, 128).transpose(2, 1, 0).reshape(128, 16)
    p[:, 48:52] = fm(inp['conv_b'][l])
    p[0:64, 52:56] = np.asarray(inp['m_norm_g'][l], np.float32).reshape(4, 64).T
    p[0:64, 56] = np.asarray(inp['sub_g'][l], np.float32)
    p[:, 57:61] = np.asarray(inp['i_bias'][l], np.float32)[None, :]
    p[:, 61:65] = np.asarray(inp['f_bias'][l], np.float32)[None, :]
    p[:, 65:73] = np.asarray(inp['sinks'][l], np.float32)[None, :]
    for k, nm in enumerate(('lam_q1', 'lam_k1', 'lam_q2', 'lam_k2')):
        p[:, 73 + 32 * k:73 + 32 * (k + 1)] = np.asarray(inp[nm][l], np.float32)[None, :]
    return p


class Arena:
    def __init__(self, nc, es, words):
        self.t = es.enter_context(nc.sbuf_tensor("arena", [128, words], F32))
        self.dry = False
        self.words = words
        self.off = 0
        self.bufs = []
        self.pending = []
        self.peak = 0

    def reset(self):
        s = set(self.pending)
        for b in self.bufs:
            if b.w is not None:
                s.add(b.w)
            s.update(b.r)
        latest = {}
        keep = []
        for ins in s:
            if ins.dma:
                keep.append(ins)
            else:
                if ins.eng not in latest or latest[ins.eng].idx < ins.idx:
                    latest[ins.eng] = ins
        self.pending = keep + list(latest.values())
        self.off = 0
        self.bufs = []

    def tile(self, free, dtype, parts=128):
        n = int(np.prod(free))
        w = n if dtype == F32 else (n + 1) // 2
        w = (w + 7) // 8 * 8
        if not self.dry:
            a = self.t[:, self.off:self.off + w]
        self.off += w
        self.peak = max(self.peak, self.off)
        if self.dry:
            return None
        assert self.off <= self.words, ("SBUF arena overflow", self.off, self.words)
        if dtype != F32:
            a = a.bitcast(dtype)
        a = a[:, 0:n]
        if len(free) == 2:
            a = a.rearrange("p (a b) -> p a b", a=free[0])
        elif len(free) == 3:
            a = a.rearrange("p (a b c) -> p a b c", a=free[0], b=free[1])
        elif len(free) == 4:
            a = a.rearrange("p (a b c d) -> p a b c d", a=free[0], b=free[1], c=free[2])
        return a

    def buf(self, name):
        b = Buf(name)
        b.r = list(self.pending)
        self.bufs.append(b)
        return b


import os as _os
MBSTOP = int(_os.environ.get('MBSTOP', '99'))
MLSTOP = int(_os.environ.get('MLSTOP', '99'))
VARX = _os.environ.get('VARX', '')


def build(S=SEQ, L=DEPTH, phases=None, arena_kb=196, dbg=False, dry=False):
    NT = S // 512
    NF = S // 256
    nc = bass.Bass("TRN2", target_bir_lowering=False)

    def dt(n, s, k="ExternalInput"):
        return nc.dram_tensor(n, s, F32, kind=k).ap()
    xT = dt("xT", [D, S])
    w_in = dt("w_in", [L, D, INW])
    w_out = dt("w_out", [L, D, D])
    w_up = dt("w_up", [L, D, DFF])
    w_down = dt("w_down", [L, DFF, D])
    par = dt("par", [L, 128, NPAR])
    cst = dt("cst", [128, NCST])
    tab = dt("tab", [4, 128, S])
    yT = dt("yT", [D, S], "ExternalOutput")
    sA = dt("sA", [D, S], "ExternalOutput" if dbg else "Internal")
    sB = dt("sB", [D, S], "Internal")
    sD = nc.dram_tensor("sD", [4, 64, S], BF16, kind="Internal").ap()
    dmix = nc.dram_tensor("dmix", [128, 8, S], BF16, kind="ExternalOutput").ap() if dbg else None
    dbufs = {}

    def dbuf(name, j):
        k = (name, j)
        if k not in dbufs:
            dbufs[k] = Buf("d_%s_%d" % k)
        return dbufs[k]

    P = Prog(nc)

    def MM(out, lhsT, rhs, start, stop, r, w):
        P.op("tensor", lambda h: h.matmul(out, lhsT=lhsT, rhs=rhs, start=start, stop=stop), r, w)

    def ACT(out, in_, func, r, w, bias=None, scale=None):
        kw = {}
        if bias is not None:
            kw['bias'] = bias
        if scale is not None:
            kw['scale'] = scale
        P.op("scalar", lambda h: h.activation(out=out, in_=in_, func=func, **kw), r, w)

    def TT(eng, out, in0, in1, op, r, w):
        P.op(eng, lambda h: h.tensor_tensor(out=out, in0=in0, in1=in1, op=op), r, w)

    def TS(eng, out, in0, s1, s2, op0, op1, r, w):
        if op1 is None:
            P.op(eng, lambda h: h.tensor_scalar(out=out, in0=in0, scalar1=s1, scalar2=None, op0=op0), r, w)
        else:
            P.op(eng, lambda h: h.tensor_scalar(out=out, in0=in0, scalar1=s1, scalar2=s2, op0=op0, op1=op1), r, w)

    def STT(out, in0, sc, in1, op0, op1, r, w):
        P.op("vector", lambda h: h.scalar_tensor_tensor(out=out, in0=in0, scalar=sc, in1=in1, op0=op0, op1=op1), r, w)

    def CP(eng, out, in_, r, w):
        P.op(eng, lambda h: h.tensor_copy(out=out, in_=in_), r, w)

    def RCP(out, in_, r, w):
        P.op("vector", lambda h: h.reciprocal(out=out, in_=in_), r, w)

    def MSET(eng, ap, val, w):
        P.op(eng, lambda h: h.memset(ap, val), (), w)

    dcount = [0]

    def DMA(out, in_, r, w, key=None):
        if key is None:
            key = "k%d" % dcount[0]
            dcount[0] += 1
        P.dma("sync", lambda h: h.dma_start(out=out, in_=in_), r, w, key)
        return key

    final_keys = []
    with ExitStack() as es:
        sb = lambda n, s, d: es.enter_context(nc.sbuf_tensor(n, s, d))
        cf = sb("cf", [128, 4, 128], F32)
        cm = sb("cm", [128, 8], F32)
        cb = sb("cb", [128, 6, 128], BF16)
        cmd = sb("cmd", [128, 4, 512], BF16)
        prm = sb("prm", [128, L, NPAR], F32)
        epsT = sb("epsT", [128, 1], F32)
        lamT = sb("lamT", [128, L, 4], F32)
        sgl = sb("sgl", [128, L], F32)
        esk = sb("esk", [128, L, 8], F32)
        pst = [es.enter_context(nc.psum_tensor("ps%d" % i, [128, 512], F32)) for i in range(8)]
        pb = [Buf("ps%d" % i) for i in range(8)]
        pb7 = [Buf("ps7_%d" % i) for i in range(4)]
        A = Arena(nc, es, arena_kb * 256)
        bC = Buf("consts")

        stg = A.tile([3080], F32)
        bS = A.buf("cstg")
        DMA(stg, cst[:, 0:3080], [], [bS])
        DMA(prm[:], par.rearrange("l p n -> p l n"), [], [bC])
        CP("vector", cf[:, 0, :], stg[:, C_ONES:C_ONES + 128], [bS], [bC])
        CP("vector", cf[:, 1, :], stg[:, C_IDENT:C_IDENT + 128], [bS], [bC])
        CP("vector", cf[:, 2, :], stg[:, C_TRIU:C_TRIU + 128], [bS], [bC])
        CP("vector", cf[:, 3, :], stg[:, C_NEG:C_NEG + 128], [bS], [bC])
        CP("vector", cm[:, 0:6], stg[:, C_RM:C_RM + 6], [bS], [bC])
        for k, off in enumerate((C_ONES, C_IDENT, C_P64, C_P32, C_MSC, C_MSP)):
            CP("vector", cb[:, k, :], stg[:, off:off + 128], [bS], [bC])
        TS("vector", cmd[:].rearrange("p a b -> p (a b)"), stg[:, C_MD:C_MD + 2048], 30000.0, -30000.0, ALU.mult, ALU.add, [bS], [bC])
        MSET("vector", epsT[:], EPS, [bC])
        ONESF, IDF, TRIU, NEGM = cf[:, 0, :], cf[:, 1, :], cf[:, 2, :], cf[:, 3, :]
        ONESB, IDB, P64B, P32B, MSC, MSP = [cb[:, k, :] for k in range(6)]
        for l in range(L):
            lam_init = 0.8 - 0.6 * math.exp(-0.3 * l)
            tmp = A.tile([64], F32)
            bt = A.buf("lamtmp")
            for k in range(2):
                TT("vector", tmp[:, 32 * k:32 * k + 32], prm[:, l, 73 + 64 * k:73 + 64 * k + 32], prm[:, l, 105 + 64 * k:105 + 64 * k + 32], ALU.mult, [bC], [bt])
                P.op("vector", lambda h, o=lamT[:, l, 2 + k:3 + k], i=tmp[:, 32 * k:32 * k + 32]: h.reduce_sum(out=o, in_=i, axis=mybir.AxisListType.X), [bt], [bC])
            ACT(lamT[:, l, 2:4], lamT[:, l, 2:4], AF.Exp, [bC], [bC])
            TT("vector", lamT[:, l, 0:1], lamT[:, l, 2:3], lamT[:, l, 3:4], ALU.subtract, [bC], [bC])
            TS("vector", lamT[:, l, 0:1], lamT[:, l, 0:1], lam_init, None, ALU.add, None, [bC], [bC])
            TS("vector", lamT[:, l, 1:2], lamT[:, l, 0:1], -1.0, None, ALU.mult, None, [bC], [bC])
            TS("vector", sgl[:, l:l + 1], prm[:, l, 56:57], 1.0 - lam_init, None, ALU.mult, None, [bC], [bC])
            ACT(esk[:, l, :], prm[:, l, 65:73], AF.Exp, [bC], [bC])

        def load_weight(dst, src_rows, ncols, scale_ap, bW, stage, bSt, cnt):
            for c0 in range(0, ncols, 2048):
                c1 = min(ncols, c0 + 2048)
                s = cnt[0] % 2
                cnt[0] += 1
                DMA(stage[s][:, 0:c1 - c0], src_rows[:, c0:c1], [], [bSt[s]], key="wst%d" % s)
                eng = "vector" if s == 0 else "gpsimd"
                if scale_ap is None:
                    CP(eng, dst[:, c0:c1], stage[s][:, 0:c1 - c0], [bSt[s]], [bW])
                else:
                    TS(eng, dst[:, c0:c1], stage[s][:, 0:c1 - c0], scale_ap, None, ALU.mult, None, [bSt[s], bC], [bW])

        def norm_rstd(ss_ps, bps, nparts, inv_n, rs_out, brs, tmp, btmp):
            ACT(tmp, ss_ps, AF.Ln, [bps, bC], [btmp], bias=epsT[0:nparts, :], scale=inv_n)
            ACT(rs_out, tmp, AF.Exp, [btmp], [brs], scale=-0.5)

        for l in range(L):
            src = xT if l == 0 else sB
            srcn = "xT" if l == 0 else "sB"
            dst_f = yT if l == L - 1 else sB
            dstn_f = "yT" if l == L - 1 else "sB"
            PR = lambda a, b, _l=l: prm[:, _l, a:b]
            if phases is None or "mix" in phases or "mixA" in phases or "mixB" in phases:
                A.reset()
                win = A.tile([8, NWA], BF16)
                bWin = A.buf("winA")
                KTd = A.tile([2, S], BF16)
                bKTdT = [A.buf("KTd%d" % k) for k in range(NT)]
                Vd = A.tile([S // 128, 4, 128], BF16)
                bVdT = [A.buf("Vd%d" % k) for k in range(NT)]
                xt = A.tile([8, 512], F32)
                bX = A.buf("x")
                qdp_raw = A.tile([2, 2048], F32)
                Qdp2 = [qdp_raw[:, k, :].bitcast(BF16).rearrange("p (a b) -> p a b", a=8) for k in range(2)]
                bQdp2 = [A.buf("Qdp0"), A.buf("Qdp1")]
                stage = [xt.rearrange("p a b -> p (a b)")[:, 0:2048], qdp_raw[:, 0, :]]
                bSt = [bX, bQdp2[0]]
                hT = A.tile([8, 512], BF16)
                bH = A.buf("hT")
                sq = A.tile([8, 512], BF16)
                bSq = A.buf("sq")
                rs = A.tile([512], F32)
                bRs = A.buf("rs")
                tm1 = A.tile([512], F32)
                bT1 = A.buf("tm1")
                tbl = A.tile([2, 512], F32)
                bTb = A.buf("tbl")
                qb = A.tile([512], BF16)
                bQb = A.buf("qb")
                t1 = A.tile([512], F32)
                t2 = A.tile([512], F32)
                bT1r, bT2r = A.buf("t1"), A.buf("t2")
                qr = A.tile([512], BF16)
                bQr = A.buf("qr")
                rd = A.tile([2, 512], F32)
                bRd = [A.buf("rd0"), A.buf("rd1")]
                sqh = A.tile([512], BF16)
                bSqh = A.buf("sqh")
                dpc = A.tile([4, 512], BF16)
                bDpc = A.buf("dpc")
                NPT = 6
                SCB = [2, 3, 7]
                Pt = A.tile([NPT, 512], BF16)
                bPt = [A.buf("Pt%d" % k) for k in range(NPT)]
                Aj = A.tile([2, 512], F32)
                bAj = [A.buf("Aj0"), A.buf("Aj1")]
                Dh = A.tile([512], F32)
                bDh = A.buf("Dh")
                cnt = [0]
                for kc in range(8):
                    load_weight(win[:, kc, :], w_in[l, kc * 128:(kc + 1) * 128, NWB:NWB + NWA], NWA, PR(kc, kc + 1), bWin, stage, bSt, cnt)
                MSET("gpsimd", Vd[:, :, :, 64:128], 1.0, bVdT)
                gen = [0]
                scn = [0]
                oac = [0]
                ptc = [0]

                def gen_bank():
                    k = gen[0] % 2
                    gen[0] += 1
                    return pst[k], pb[k]

                def sc_bank():
                    k = 2 + scn[0] % 2
                    scn[0] += 1
                    return pst[k], pb[k]

                def o_bank():
                    k = 4 + oac[0] % 2
                    oac[0] += 1
                    return pst[k], pb[k]

                def prologue(i, xt, bX, tbl, bTb, tsl, hT, bH, sq, bSq, rs, bRs, tm1, bT1):
                    t0 = i * 512
                    DMA(xt, src.rearrange("(k p) t -> p k t", p=128)[:, :, t0:t0 + 512],
                        [dbuf(srcn, 2 * i), dbuf(srcn, 2 * i + 1)], [bX], key="xld")
                    DMA(tbl, tab.rearrange("f p t -> p f t")[:, tsl, t0:t0 + 512], [], [bTb], key="tbl")
                    ACT(sq.rearrange("p a b -> p (a b)"), xt.rearrange("p a b -> p (a b)"), AF.Square, [bX], [bSq])
                    for kc in range(8):
                        MM(pst[6][:, :], ONESB, sq[:, kc, :], kc == 0, kc == 7, [bSq, bC], [pb[6]])
                    norm_rstd(pst[6][:, :], pb[6], 128, 1.0 / D, rs, bRs, tm1, bT1)
                    TT("vector", hT[:, 0:4, :], xt[:, 0:4, :], rs.unsqueeze(1).broadcast_to([128, 4, 512]), ALU.mult, [bX, bRs], [bH])
                    TT("gpsimd", hT[:, 4:8, :], xt[:, 4:8, :], rs.unsqueeze(1).broadcast_to([128, 4, 512]), ALU.mult, [bX, bRs], [bH])

                def inproj_fm(col0, m=128):
                    pt, pbuf = gen_bank()
                    for kc in range(8):
                        MM(pt[0:m, :], win[:, kc, col0:col0 + m], hT[:, kc, :], kc == 0, kc == 7, [bWin, bH], [pbuf])
                    return pt, pbuf

                def rope_chunk(col0, permb, dest, bdest):
                    pt, pbuf = inproj_fm(col0)
                    CP("vector", qb, pt[:, :], [pbuf], [bQb])
                    MM(pst[6][:, :], permb, qb, True, True, [bQb, bC], [pb[6]])
                    TT("vector", t1, pt[:, :], tbl[:, 0, :], ALU.mult, [pbuf, bTb], [bT1r])
                    TT("vector", t2, pst[6][:, :], tbl[:, 1, :], ALU.mult, [pb[6], bTb], [bT2r])
                    TT("gpsimd", dest, t1, t2, ALU.add, [bT1r, bT2r], [bdest])

                def f_units(i):
                    t0 = i * 512
                    qs = i % 2
                    us = []
                    us.append(lambda: prologue(i, xt, bX, tbl, bTb, slice(2, 4), hT, bH, sq, bSq, rs, bRs, tm1, bT1))

                    def uq(c):
                        rope_chunk(A_DQ + c * 128, P32B, qr, bQr)
                        for jj in range(4):
                            TS("gpsimd", Qdp2[qs][:, 4 * c + jj, :], qr, cm[:, jj:jj + 1], None, ALU.mult, None, [bQr, bC], [bQdp2[qs]])

                    def uk(c):
                        rope_chunk(A_DK + c * 128, P32B, KTd[:, c, t0:t0 + 512], bKTdT[i])

                    def uv(bi):
                        pt, pbuf = gen_bank()
                        for kc in range(8):
                            MM(pt[:, 0:256], hT[:, kc, bi * 128:(bi + 1) * 128], win[:, kc, A_DV:A_DV + 256], kc == 0, kc == 7, [bWin, bH], [pbuf])
                        CP("vector", Vd[:, 4 * i + bi, :, 0:64], pt[:, 0:256].rearrange("p (h d) -> p h d", h=4), [pbuf], [bVdT[i]])
                    for c in range(2):
                        us.append(lambda c=c: uq(c))
                    for c in range(2):
                        us.append(lambda c=c: uk(c))
                    for bi in range(4):
                        us.append(lambda bi=bi: uv(bi))
                    return us

                NTA = NT if (phases is None or 'mix' in phases or 'mixA' in phases) else 0
                if NTA:
                    for u in f_units(0):
                        u()
                for i in range(NTA):
                    t0 = i * 512
                    Qdp = Qdp2[i % 2]
                    bQdp = bQdp2[i % 2]
                    nxt = f_units(i + 1) if i + 1 < NTA else []
                    nxt_all = list(nxt)
                    nkb = 4 * (i + 1)
                    LAG = 2
                    n_items = 8 * nkb
                    item = [0]
                    pend = []
                    deferred = []

                    def tick():
                        for d_ in deferred:
                            d_[0] -= 1
                        while deferred and deferred[0][0] <= 0:
                            deferred.pop(0)[1]()

                    def fin_a(j, ot, obuf):
                        h4 = j // 2
                        jp = j % 2
                        RCP(rd[0:64, jp, :], ot[64:128, :], [obuf], [bRd[jp]])
                        TT("vector", Aj[0:64, jp, :], ot[0:64, :], rd[0:64, jp, :], ALU.mult, [obuf, bRd[jp]], [bAj[jp]])
                        if jp == 1:
                            STT(Dh[0:64, :], Aj[0:64, 1, :], lamT[0:64, l, 1:2], Aj[0:64, 0, :], ALU.mult, ALU.add, [bAj[0], bAj[1], bC], [bDh])
                            TT("gpsimd", sqh[0:64, :], Dh[0:64, :], Dh[0:64, :], ALU.mult, [bDh], [bSqh])

                            def part_b(h4=h4):
                                MM(pst[6][0:64, :], ONESB[0:64, 0:64], sqh[0:64, :], True, True, [bSqh, bC], [pb[6]])
                                norm_rstd(pst[6][0:64, :], pb[6], 64, 1.0 / 64, rs[0:64, :], bRs, tm1[0:64, :], bT1)
                                STT(dpc[0:64, h4, :], Dh[0:64, :], sgl[0:64, l:l + 1], rs[0:64, :], ALU.mult, ALU.mult, [bDh, bRs, bC], [bDpc])
                            deferred.append([5, part_b])

                    def stage2(j, kb, k3, ot, obuf):
                        MM(ot[:, :], Vd[:, kb, j // 2, :], Pt[:, k3, :], kb == 0, kb == nkb - 1, [bVdT[kb // 4], bPt[k3]], [obuf])
                        if kb == nkb - 1:
                            fin_a(j, ot, obuf)

                    for j in range(8):
                        c = j // 4
                        ot, obuf = o_bank()
                        for kb in range(nkb):
                            sk_ = SCB[scn[0] % len(SCB)]
                            scn[0] += 1
                            st, sbuf_ = pst[sk_], pb[sk_]
                            diag = kb >= 4 * i
                            MM(st[:, :], KTd[:, c, kb * 128:(kb + 1) * 128], Qdp[:, j, :], True, not diag, [bKTdT[kb // 4], bQdp], [sbuf_])
                            if diag:
                                MM(st[:, :], IDB, cmd[:, kb - 4 * i, :], False, True, [bC], [sbuf_])
                            k3 = ptc[0] % NPT
                            ptc[0] += 1
                            ACT(Pt[:, k3, :], st[:, :], AF.Exp, [sbuf_], [bPt[k3]], scale=32.0 ** -0.5)
                            pend.append((j, kb, k3, ot, obuf))
                            if len(pend) > LAG:
                                stage2(*pend.pop(0))
                            tick()
                            item[0] += 1
                            if nxt and item[0] * (len(nxt_all) + 1) >= (len(nxt_all) - len(nxt) + 1) * n_items:
                                nxt.pop(0)()
                    while pend:
                        stage2(*pend.pop(0))
                    while deferred:
                        deferred.pop(0)[1]()
                    while nxt:
                        nxt.pop(0)()
                    DMA(sD[:, :, t0:t0 + 512].rearrange("h p t -> p h t"), dpc[0:64, :, :], [bDpc], [dbuf("sD", i)], key="dst")

            if phases is None or "mix" in phases or "mixB" in phases:
                A.reset()
                win = A.tile([8, NWB], BF16)
                wout = A.tile([8, D], BF16)
                bWin, bWout = A.buf("win"), A.buf("wout")
                xt = A.tile([8, 512], F32)
                bX = A.buf("x")
                yv = A.tile([8, 512], F32)
                bY = A.buf("y")
                stage = [xt.rearrange("p a b -> p (a b)")[:, 0:2048], yv.rearrange("p a b -> p (a b)")[:, 0:2048]]
                bSt = [bX, bY]
                hT = A.tile([8, 512], BF16)
                bH = A.buf("hT")
                sq = A.tile([8, 512], BF16)
                bSq = A.buf("sq")
                rs = A.tile([512], F32)
                bRs = A.buf("rs")
                tm1 = A.tile([512], F32)
                bT1 = A.buf("tm1")
                tbl = A.tile([2, 512], F32)
                bTb = A.buf("tbl")
                pre = A.tile([515], F32)
                halo = A.tile([4, 3], F32)
                bPre = A.buf("pre")
                bHalo = A.buf("halo")
                acc = A.tile([512], F32)
                bAcc = A.buf("acc")
                qm = A.tile([2, 512], BF16)
                km = A.tile([2, 512], BF16)
                bQm, bKm = A.buf("qm"), A.buf("km")
                qmp = A.tile([4, 512], BF16)
                bQmp = A.buf("qmp")
                og = A.tile([4, 512], BF16)
                bOg = A.buf("og")
                qb = A.tile([512], BF16)
                bQb = A.buf("qb")
                t1 = A.tile([512], F32)
                t2 = A.tile([512], F32)
                bT1r, bT2r = A.buf("t1"), A.buf("t2")
                qr = A.tile([512], BF16)
                bQr = A.buf("qr")
                QTs = A.tile([4, 512], BF16)
                bQTs = A.buf("QTs")
                KTs = A.tile([2, 640], BF16)
                bKTs = A.buf("KTs")
                Vs = A.tile([5, 2, 128], BF16)
                bVs = A.buf("Vs")
                Vm = A.tile([4, 4, 128], BF16)
                bVm = A.buf("Vm")
                gt = A.tile([4, 8], F32)
                bGt = A.buf("gt")
                lft = A.tile([4, 4], F32)
                bLf = A.buf("lft")
                gtmp = A.tile([4, 4], F32)
                bGtmp = A.buf("gtmp")
                u4d = A.tile([2, 4], F32)
                w4d = A.tile([2, 4], F32)
                decd = A.tile([2, 4], F32)
                LFbd = A.tile([2, 4, 128], F32)
                EBd = A.tile([2, 4, 128], F32)
                DTd = A.tile([2, 4, 128], F32)
                scd = A.tile([2, 4, 128], BF16)
                Qsd = A.tile([2, 4, 128], BF16)
                mlb = {nm: [A.buf(nm + "0"), A.buf(nm + "1")] for nm in ("u4", "w4", "dec", "LFb", "EB", "DT", "sc", "Qs")}
                Kw4 = A.tile([256], BF16)
                bKw = A.buf("Kw")
                C32 = A.tile([4, 128], F32)
                Cbf = A.tile([4, 128], BF16)
                bC32A = A.buf("C32")
                bCbfA = A.buf("Cbf")
                dn = A.tile([2, 512], F32)
                rd = A.tile([512], F32)
                bDn, bRd = A.buf("dn"), A.buf("rd")
                hmt = A.tile([4, 512], F32)
                bHm = [A.buf("hm%d" % h) for h in range(4)]
                sqh = A.tile([512], BF16)
                bSqh = A.buf("sqh")
                tmph = A.tile([512], F32)
                bTmph = A.buf("tmph")
                pieceb = A.tile([512], BF16)
                bPiece = A.buf("piece")
                mix = A.tile([8, 512], BF16)
                bMix = A.buf("mix")
                bMixD = A.buf("mixD")
                NPT = 4
                Pt = A.tile([NPT, 512], BF16)
                bPt = [A.buf("Pt%d" % k) for k in range(NPT)]
                cnt = [0]
                for kc in range(8):
                    load_weight(win[:, kc, :], w_in[l, kc * 128:(kc + 1) * 128, 0:NWB], NWB, PR(kc, kc + 1), bWin, stage, bSt, cnt)
                for kc in range(8):
                    load_weight(wout[:, kc, :], w_out[l, kc * 128:(kc + 1) * 128, :], D, None, bWout, stage, bSt, cnt)
                MSET("gpsimd", halo[:], 0.0, [bHalo])
                MSET("gpsimd", KTs[:], 0.0, [bKTs])
                MSET("gpsimd", Vs[:], 0.0, [bVs])
                MSET("gpsimd", Vs[:, :, :, 64:128], 1.0, [bVs])
                MSET("gpsimd", Vm[:, :, :, 64:128], 1.0, [bVm])
                MSET("vector", C32[:], 0.0, [bC32A])
                MSET("vector", Cbf[:], 0.0, [bCbfA])
                gen = [0]
                scn = [0]
                oac = [0]
                ptc = [0]

                for i in range(NT):
                    t0 = i * 512
                    prologue(i, xt, bX, tbl, bTb, slice(0, 2), hT, bH, sq, bSq, rs, bRs, tm1, bT1)
                    for h4 in range(4):
                        DMA(mix[(h4 % 2) * 64:(h4 % 2) * 64 + 64, 6 + h4 // 2, :], sD[h4, :, t0:t0 + 512], [dbuf("sD", i)], [bMixD], key="dld")
                    if MBSTOP < 1:
                        continue
                    for c in range(4):
                        pt, pbuf = inproj_fm(c * 128)
                        CP("vector", pre[:, 0:3], halo[:, c, :], [bHalo], [bPre])
                        CP("vector", pre[:, 3:515], pt[:, :], [pbuf], [bPre])
                        CP("vector", halo[:, c, :], pre[:, 512:515], [bPre], [bHalo])
                        TS("vector", acc, pre[:, 3:515], PR(32 + c * 4 + 3, 32 + c * 4 + 4), PR(48 + c, 49 + c), ALU.mult, ALU.add, [bPre, bC], [bAcc])
                        for j in (2, 1, 0):
                            STT(acc, pre[:, j:j + 512], PR(32 + c * 4 + j, 32 + c * 4 + j + 1), acc, ALU.mult, ALU.add, [bPre, bAcc, bC], [bAcc])
                        if c < 2:
                            ACT(qm[:, c, :], acc, AF.Silu, [bAcc], [bQm])
                            for hh in range(2):
                                TS("gpsimd", qmp[:, 2 * c + hh, :], qm[:, c, :], cm[:, 4 + hh:5 + hh], None, ALU.mult, None, [bQm, bC], [bQmp])
                        else:
                            ACT(t1, acc, AF.Silu, [bAcc], [bT1r])
                            TS("gpsimd", km[:, c - 2, :], t1, 0.125, None, ALU.mult, None, [bT1r], [bKm])
                    if MBSTOP < 2:
                        continue
                    for h4 in range(4):
                        pt, pbuf = inproj_fm(FM_MO + h4 * 64, m=64)
                        ACT(og[0:64, h4, :], pt[0:64, :], AF.Sigmoid, [pbuf], [bOg])
                    if MBSTOP < 3:
                        continue
                    for c in range(4):
                        rope_chunk(FM_SQ + c * 128, P64B, QTs[:, c, :], bQTs)
                    rope_chunk(FM_SK, P64B, qr, bQr)
                    for g in range(2):
                        TS("gpsimd", KTs[:, g, 128:640], qr, cm[:, 4 + g:5 + g], None, ALU.mult, None, [bQr, bC], [bKTs])
                    if MBSTOP < 4:
                        continue
                    for bi in range(4):
                        pt, pbuf = gen_bank()
                        for kc in range(8):
                            MM(pt[:, 0:392], hT[:, kc, bi * 128:(bi + 1) * 128], win[:, kc, TMB0:TMB0 + 392], kc == 0, kc == 7, [bWin, bH], [pbuf])
                        CP("vector", Vm[:, bi, :, 0:64], pt[:, 0:256].rearrange("p (h d) -> p h d", h=4), [pbuf], [bVm])
                        CP("vector", Vs[:, 1 + bi, :, 0:64], pt[:, 256:384].rearrange("p (g d) -> p g d", g=2), [pbuf], [bVs])
                        TT("vector", gt[:, bi, :], pt[:, 384:392], PR(57, 65), ALU.add, [pbuf, bC], [bGt])
                    TS("vector", gtmp, gt[:, :, 4:8], -1.0, None, ALU.mult, None, [bGt], [bGtmp])
                    ACT(gtmp.rearrange("p a b -> p (a b)"), gtmp.rearrange("p a b -> p (a b)"), AF.Exp, [bGtmp], [bGtmp])
                    TS("vector", gtmp, gtmp, 1.0, None, ALU.add, None, [bGtmp], [bGtmp])
                    ACT(gtmp.rearrange("p a b -> p (a b)"), gtmp.rearrange("p a b -> p (a b)"), AF.Ln, [bGtmp], [bGtmp])
                    TS("vector", lft, gtmp, -1.0, None, ALU.mult, None, [bGtmp], [bLf])

                    def ml_s1(bi):
                        blk = slice(bi * 128, (bi + 1) * 128)
                        s_ = bi % 2
                        u4, w4, dec = u4d[:, s_, :], w4d[:, s_, :], decd[:, s_, :]
                        LFb4, EB4, DT4, sc4, Qs4 = LFbd[:, s_], EBd[:, s_], DTd[:, s_], scd[:, s_], Qsd[:, s_]
                        bU4, bW4, bDec, bLFb, bEB, bDT, bSc, bQs = [mlb[nm][s_] for nm in ("u4", "w4", "dec", "LFb", "EB", "DT", "sc", "Qs")]
                        MM(pst[6][:, 0:4], TRIU, lft[:, bi, :], True, True, [bLf, bC], [pb[6]])
                        TT("vector", u4, gt[:, bi, 0:4], pst[6][:, 0:4], ALU.subtract, [bGt, pb[6]], [bU4])
                        CP("vector", LFb4, lft[:, bi, :].unsqueeze(2).broadcast_to([128, 4, 128]), [bLf], [bLFb])
                        for h in range(4):
                            MM(pst[0][:, h * 128:(h + 1) * 128], LFb4[:, h, :], TRIU, True, True, [bLFb, bC], [pb[0]])
                        for h in range(4):
                            MM(pst[1][:, h * 128:(h + 1) * 128], LFb4[:, h, :], TRIU, True, False, [bLFb, bC], [pb[1]])
                            MM(pst[1][:, h * 128:(h + 1) * 128], IDF, NEGM, False, True, [bC], [pb[1]])
                        ACT(EB4.rearrange("p a b -> p (a b)"), pst[0][:, :], AF.Exp, [pb[0]], [bEB])
                        for h in range(4):
                            ACT(DT4[:, h, :], pst[1][:, h * 128:(h + 1) * 128], AF.Exp, [pb[1], bU4], [bDT], bias=u4[:, h:h + 1])
                        CP("vector", w4, DT4[:, :, 127], [bDT], [bW4])
                        CP("vector", dec, EB4[:, :, 127], [bEB], [bDec])
                        for h in range(4):
                            MM(pst[7][:, h * 128:(h + 1) * 128], km[:, h // 2, blk], qmp[:, h, blk], True, True, [bKm, bQmp], [pb[7]])
                        TT("vector", sc4.rearrange("p a b -> p (a b)"), pst[7][:, :], DT4.rearrange("p a b -> p (a b)"), ALU.mult, [pb[7], bDT], [bSc])
                        TT("gpsimd", Qs4, qmp[:, :, blk], EB4, ALU.mult, [bQmp, bEB], [bQs])

                    def ml_s2(bi):
                        blk = slice(bi * 128, (bi + 1) * 128)
                        s_ = bi % 2
                        u4, w4, dec = u4d[:, s_, :], w4d[:, s_, :], decd[:, s_, :]
                        LFb4, EB4, DT4, sc4, Qs4 = LFbd[:, s_], EBd[:, s_], DTd[:, s_], scd[:, s_], Qsd[:, s_]
                        bU4, bW4, bDec, bLFb, bEB, bDT, bSc, bQs = [mlb[nm][s_] for nm in ("u4", "w4", "dec", "LFb", "EB", "DT", "sc", "Qs")]
                        for h in range(4):
                            MM(pst[6][:, h * 128:(h + 1) * 128], Vm[:, bi, h, :], sc4[:, h, :], True, False, [bVm, bSc], [pb[6]])
                            MM(pst[6][:, h * 128:(h + 1) * 128], Cbf[:, h, :], Qs4[:, h, :], False, True, [bCbfA, bQs], [pb[6]])
                        CP("vector", dn[0:64, 0, :], pst[6][64:128, :], [pb[6]], [bDn])
                        STT(dn[0:64, 1, :], dn[0:64, 0, :], -1.0, dn[0:64, 0, :], ALU.mult, ALU.max, [bDn], [bDn])
                        TS("vector", dn[0:64, 0, :], dn[0:64, 1, :], 1.0, None, ALU.max, None, [bDn], [bDn])
                        ACT(dn[0:64, 1, :], dn[0:64, 0, :], AF.Ln, [bDn], [bDn])
                        ACT(rd[0:64, :], dn[0:64, 1, :], AF.Exp, [bDn], [bRd], scale=-1.0)
                        TT("vector", hmt[0:64, :, blk], pst[6][0:64, :].rearrange("p (h t) -> p h t", h=4), rd[0:64, :].rearrange("p (h t) -> p h t", h=4),
                           ALU.mult, [pb[6], bRd], bHm)
                        for c in range(2):
                            MM(pst[7][:, c * 128:(c + 1) * 128], km[:, c, blk], IDB, True, True, [bKm, bC], [pb[7]])
                        TT("vector", Kw4.rearrange("p (h d) -> p h d", h=4), pst[7][:, 0:256].rearrange("p (h d) -> p h d", h=4),
                           w4.unsqueeze(2).broadcast_to([128, 4, 64]), ALU.mult, [pb[7], bW4], [bKw])
                        for h in range(4):
                            MM(pst[0][:, h * 128:(h + 1) * 128], Kw4[:, (h // 2) * 128:(h // 2 + 1) * 128], Vm[:, bi, h, :], True, True, [bKw, bVm], [pb[0]])
                        dC = pst[0][:, :].rearrange("p (h t) -> p h t", h=4)
                        for par in range(2):
                            r5 = slice(par * 64, par * 64 + 64)
                            TT("vector", C32[r5, par::2, :], C32[r5, par::2, :], dec[r5, par::2].unsqueeze(2).broadcast_to([64, 2, 128]), ALU.mult, [bC32A, bDec], [bC32A])
                            TT("vector", C32[r5, par::2, :], C32[r5, par::2, :], dC[r5, par::2, :], ALU.add, [bC32A, pb[0]], [bC32A])
                            CP("vector", Cbf[r5, par::2, :], C32[r5, par::2, :], [bC32A], [bCbfA])
                        hmb = hmt[0:64, :, blk]
                        v4 = lambda ap: ap.rearrange("p (h t) -> p h t", h=4)
                        TT("gpsimd", v4(sqh[0:64, :]), hmb, hmb, ALU.mult, bHm, [bSqh])
                        MM(pst[6][0:64, :], ONESB[0:64, 0:64], sqh[0:64, :], True, True, [bSqh, bC], [pb[6]])
                        norm_rstd(pst[6][0:64, :], pb[6], 64, 1.0 / 64, rs[0:64, :], bRs, tm1[0:64, :], bT1)
                        TT("vector", v4(tmph[0:64, :]), hmb, v4(rs[0:64, :]), ALU.mult, bHm + [bRs], [bTmph])
                        TT("vector", v4(tmph[0:64, :]), v4(tmph[0:64, :]), PR(52, 56)[0:64, :].unsqueeze(2).broadcast_to([64, 4, 128]), ALU.mult, [bTmph, bC], [bTmph])
                        TT("gpsimd", v4(pieceb[0:64, :]), v4(tmph[0:64, :]), og[0:64, :, blk], ALU.mult, [bTmph, bOg], [bPiece])
                        CP("vector", mix[0:64, 0:2, blk], v4(pieceb[0:64, :])[:, 0::2, :], [bPiece], [bMix])
                        CP("vector", mix[64:128, 0:2, blk], v4(pieceb[0:64, :])[:, 1::2, :], [bPiece], [bMix])

                    def swa_fin(bi, g, ot, obuf):
                        CP("vector", dn[0:64, 0, :], ot[64:128, :], [obuf], [bDn])
                        TT("vector", dn[0:64, 1, :].rearrange("p (h q) -> p h q", h=4), dn[0:64, 0, :].rearrange("p (h q) -> p h q", h=4),
                           esk[0:64, l, 4 * g:4 * g + 4].unsqueeze(2).broadcast_to([64, 4, 128]), ALU.add, [bDn, bC], [bDn])
                        ACT(dn[0:64, 0, :], dn[0:64, 1, :], AF.Ln, [bDn], [bDn])
                        ACT(rd[0:64, :], dn[0:64, 0, :], AF.Exp, [bDn], [bRd], scale=-1.0)
                        TT("vector", pieceb[0:64, :], ot[0:64, :], rd[0:64, :], ALU.mult, [obuf, bRd], [bPiece])
                        pv = pieceb[0:64, :].rearrange("p (c e q) -> p c e q", c=2, e=2)
                        CP("vector", mix[0:64, 2 + 2 * g:4 + 2 * g, bi * 128:(bi + 1) * 128], pv[:, :, 0, :], [bPiece], [bMix])
                        CP("vector", mix[64:128, 2 + 2 * g:4 + 2 * g, bi * 128:(bi + 1) * 128], pv[:, :, 1, :], [bPiece], [bMix])

                    def swa_s2(bi, g, cur, k3, first, last, ot, obuf):
                        MM(ot[:, :], Vs[:, bi + cur, g, :], Pt[:, k3, :], first, last, [bVs, bPt[k3]], [obuf])
                        if last:
                            swa_fin(bi, g, ot, obuf)

                    def swa_block(bi):
                        pend = []
                        n = 4 * i + bi
                        for g in range(2):
                            ot, obuf = o_bank()
                            kbs = ([0] if n > 0 else []) + [1]
                            for kk, cur in enumerate(kbs):
                                st, sbuf_ = sc_bank()
                                kcol = (bi + cur) * 128
                                MM(st[:, :], KTs[:, g, kcol:kcol + 128], QTs[:, :, bi * 128:(bi + 1) * 128], True, True, [bKTs, bQTs], [sbuf_])
                                k3 = ptc[0] % NPT
                                ptc[0] += 1
                                ACT(Pt[:, k3, :], st[:, :], AF.Exp, [sbuf_], [bPt[k3]], scale=0.125)
                                msk = MSC if cur else MSP
                                TT("gpsimd", Pt[:, k3, :].rearrange("p (h q) -> p h q", h=4), Pt[:, k3, :].rearrange("p (h q) -> p h q", h=4),
                                   msk.unsqueeze(1).broadcast_to([128, 4, 128]), ALU.mult, [bPt[k3], bC], [bPt[k3]])
                                pend.append((bi, g, cur, k3, kk == 0, kk == len(kbs) - 1, ot, obuf))
                                if len(pend) > 1:
                                    swa_s2(*pend.pop(0))
                        while pend:
                            swa_s2(*pend.pop(0))

                    ml_s1(0)
                    for bi in range(4):
                        if bi + 1 < 4:
                            ml_s1(bi + 1)
                        swa_block(bi)
                        ml_s2(bi)

                    CP("gpsimd", KTs[:, :, 0:128], KTs[:, :, 512:640], [bKTs], [bKTs])
                    CP("gpsimd", Vs[:, 0, :, 0:64], Vs[:, 4, :, 0:64], [bVs], [bVs])

                    if MBSTOP < 8:
                        continue
                    if dbg and l == 0:
                        DMA(dmix[:, :, t0:t0 + 512], mix, [bMix, bMixD], [], key="dbgmix")
                    for oc in range(8):
                        pt, pbuf = gen_bank()
                        for kc in range(8):
                            MM(pt[:, :], wout[:, kc, oc * 128:(oc + 1) * 128], mix[:, kc, :], kc == 0, kc == 7, [bWout, bMix, bMixD], [pbuf])
                        CP("vector", yv[:, oc, :], pt[:, :], [pbuf], [bY])
                    TT("gpsimd", sq.rearrange("p a b -> p (a b)"), yv.rearrange("p a b -> p (a b)"), yv.rearrange("p a b -> p (a b)"), ALU.mult, [bY], [bSq])
                    for kc in range(8):
                        MM(pst[6][:, :], ONESB, sq[:, kc, :], kc == 0, kc == 7, [bSq, bC], [pb[6]])
                    norm_rstd(pst[6][:, :], pb[6], 128, 1.0 / D, rs, bRs, tm1, bT1)
                    for oc in range(8):
                        STT(yv[:, oc, :], yv[:, oc, :], PR(8 + oc, 9 + oc), rs, ALU.mult, ALU.mult, [bY, bRs, bC], [bY])
                    TT("gpsimd", yv.rearrange("p a b -> p (a b)"), yv.rearrange("p a b -> p (a b)"), xt.rearrange("p a b -> p (a b)"), ALU.add, [bY, bX], [bY])
                    DMA(sA.rearrange("(k p) t -> p k t", p=128)[:, :, t0:t0 + 512], yv, [bY], [dbuf("sA", 2 * i), dbuf("sA", 2 * i + 1)], key="xst")

            if phases is None or "ffn" in phases:
                fsrc, fsrcn = (sA, "sA") if (phases is None or "mix" in phases) else (src, srcn)
                A.reset()
                wup = A.tile([8, DFF], BF16)
                wdn = A.tile([32, D], BF16)
                bWup, bWdn = A.buf("wup"), A.buf("wdn")
                stg2 = A.tile([2, 2048], F32)
                stage = [stg2[:, 0, :], stg2[:, 1, :]]
                bSt = [A.buf("st0"), A.buf("st1")]
                xt2 = A.tile([2, 8, 256], F32)
                bX2 = [A.buf("x0"), A.buf("x1")]
                hT2 = A.tile([2, 8, 256], BF16)
                bH2 = [A.buf("hT0"), A.buf("hT1")]
                sqp = A.tile([8, 256], BF16)
                bSqp = A.buf("sqp")
                sqo = A.tile([8, 256], BF16)
                bSqo = A.buf("sqo")
                rs2 = A.tile([2, 256], F32)
                bRs2 = [A.buf("rs0"), A.buf("rs1")]
                rso = A.tile([256], F32)
                bRso = A.buf("rso")
                tm1 = A.tile([256], F32)
                bT1 = A.buf("tm1")
                tm2 = A.tile([256], F32)
                bT2 = A.buf("tm2")
                rl = A.tile([2, 256], F32)
                bRl = [A.buf("rl0"), A.buf("rl1")]
                u2 = stg2.rearrange("p a b -> p (a b)").bitcast(BF16).rearrange("p (a b) -> p a b", a=32)
                bU2 = A.buf("u2")
                yv = A.tile([8, 256], F32)
                bY = A.buf("y")
                cnt = [0]
                for kc in range(8):
                    load_weight(wup[:, kc, :], w_up[l, kc * 128:(kc + 1) * 128, :], DFF, PR(16 + kc, 17 + kc), bWup, stage, bSt, cnt)
                for fc in range(32):
                    load_weight(wdn[:, fc, :], w_down[l, fc * 128:(fc + 1) * 128, :], D, None, bWdn, stage, bSt, cnt)
                gen = [0]

                def ffn_pro(i):
                    t0 = i * 256
                    xs = i % 2
                    xt = xt2[:, xs]
                    DMA(xt, fsrc.rearrange("(k p) t -> p k t", p=128)[:, :, t0:t0 + 256], [dbuf(fsrcn, i)], [bX2[xs]], key="xld%d" % xs)
                    ACT(sqp.rearrange("p a b -> p (a b)"), xt.rearrange("p a b -> p (a b)"), AF.Square, [bX2[xs]], [bSqp])
                    for kc in range(8):
                        MM(pst[6][:, 0:256], ONESB, sqp[:, kc, :], kc == 0, kc == 7, [bSqp, bC], [pb[6]])
                    norm_rstd(pst[6][:, 0:256], pb[6], 128, 1.0 / D, rs2[:, xs, :], bRs2[xs], tm1, bT1)
                    TT("vector", hT2[:, xs, 0:4, :], xt[:, 0:4, :], rs2[:, xs, :].unsqueeze(1).broadcast_to([128, 4, 256]), ALU.mult, [bX2[xs], bRs2[xs]], [bH2[xs]])
                    TT("gpsimd", hT2[:, xs, 4:8, :], xt[:, 4:8, :], rs2[:, xs, :].unsqueeze(1).broadcast_to([128, 4, 256]), ALU.mult, [bX2[xs], bRs2[xs]], [bH2[xs]])

                def ffn_post(i):
                    nonlocal_keys = final_keys
                    t0 = i * 256
                    xs = i % 2
                    xt = xt2[:, xs]
                    for kc in range(8):
                        MM(pst[7][:, 0:256], ONESB, sqo[:, kc, :], kc == 0, kc == 7, [bSqo, bC], [pb[7]])
                    norm_rstd(pst[7][:, 0:256], pb[7], 128, 1.0 / D, rso, bRso, tm2, bT2)
                    for oc in range(8):
                        STT(yv[:, oc, :], yv[:, oc, :], PR(24 + oc, 25 + oc), rso, ALU.mult, ALU.mult, [bY, bRso, bC], [bY])
                    TT("gpsimd", yv.rearrange("p a b -> p (a b)"), yv.rearrange("p a b -> p (a b)"), xt.rearrange("p a b -> p (a b)"), ALU.add, [bY, bX2[xs]], [bY])
                    key = DMA(dst_f.rearrange("(k p) t -> p k t", p=128)[:, :, t0:t0 + 256], yv, [bY], [dbuf(dstn_f, i)], key="xst_%s" % dstn_f)
                    if key not in nonlocal_keys:
                        nonlocal_keys.append(key)

                ffn_pro(0)
                for i in range(NF):
                    xs = i % 2
                    for fc in range(32):
                        if fc == 6 and i > 0:
                            ffn_post(i - 1)
                        k = gen[0] % 4
                        gen[0] += 1
                        pt = pst[k][:, 0:256]
                        pbuf = pb[k]
                        for kc in range(8):
                            MM(pt, wup[:, kc, fc * 128:(fc + 1) * 128], hT2[:, xs, kc, :], kc == 0, kc == 7, [bWup, bH2[xs]], [pbuf])
                        r2 = fc % 2
                        if fc % 4 == 3:
                            TS("vector", rl[:, r2, :], pt, 0.0, None, ALU.max, None, [pbuf], [bRl[r2]])
                        else:
                            ACT(rl[:, r2, :], pt, AF.Relu, [pbuf], [bRl[r2]])
                        TT("gpsimd", u2[:, fc, :], rl[:, r2, :], rl[:, r2, :], ALU.mult, [bRl[r2]], [bU2] + (bSt if i == 0 else []))
                    if i + 1 < NF:
                        ffn_pro(i + 1)
                    for oc in range(8):
                        k = 4 + oc % 2
                        pt, pbuf = pst[k][:, 0:256], pb[k]
                        for fc in range(32):
                            MM(pt, wdn[:, fc, oc * 128:(oc + 1) * 128], u2[:, fc, :], fc == 0, fc == 31, [bWdn, bU2], [pbuf])
                        CP("vector", yv[:, oc, :], pt, [pbuf], [bY])
                    TT("gpsimd", sqo.rearrange("p a b -> p (a b)"), yv.rearrange("p a b -> p (a b)"), yv.rearrange("p a b -> p (a b)"), ALU.mult, [bY], [bSqo])
                ffn_post(NF - 1)
        P.emit(final_waits=[("dma", k) for k in final_keys])
    return nc, P, A


_CACHE = {}


def kernel(**inputs):
    x = np.asarray(inputs['x'], np.float32)
    B, S, _ = x.shape
    L = inputs['w_in'].shape[0]
    perm = _perm_in()
    cstv, tabs = _consts(S)
    w_in = np.ascontiguousarray(np.asarray(inputs['w_in'], np.float32)[:, :, perm])
    shared = {
        "w_in": w_in,
        "w_out": np.ascontiguousarray(np.asarray(inputs['w_out'], np.float32)),
        "w_up": np.ascontiguousarray(np.asarray(inputs['w_up'], np.float32)),
        "w_down": np.ascontiguousarray(np.asarray(inputs['w_down'], np.float32)),
        "par": np.stack([_params(inputs, l) for l in range(L)]),
        "cst": cstv, "tab": tabs,
    }
    in_maps = []
    for b in range(B):
        m = dict(shared)
        m["xT"] = np.ascontiguousarray(x[b].T)
        in_maps.append(m)
    nc, P, A = build(S, L)
    res = run_bass_kernel_spmd(nc, in_maps, core_ids=list(range(B)))
    out = np.stack([np.ascontiguousarray(r["yT"].T) for r in res.results], axis=0)
    return out.astype(np.float32)
```

```python
import math
import numpy as np
import concourse.bass as bass
import concourse.mybir as mybir
from concourse.bass_utils import run_bass_kernel_spmd
from contextlib import ExitStack

F32 = mybir.dt.float32
BF16 = mybir.dt.bfloat16
AF = mybir.ActivationFunctionType
ALU = mybir.AluOpType

D = 1024
SEQ = 8192
NB = 8
DEPTH = 2
DFF = 4096
INW = 2568
EPS = 1e-6
class Buf:
    __slots__ = ("name", "w", "r")

    def __init__(self, name):
        self.name = name
        self.w = None
        self.r = []


class Ins:
    __slots__ = ("eng", "fn", "deps", "sig", "cnt", "dma", "sem", "semval", "idx", "raw", "pos")

    def __init__(self, eng, fn, dma=False):
        self.eng = eng
        self.fn = fn
        self.deps = []
        self.sig = False
        self.cnt = 0
        self.dma = dma
        self.sem = None
        self.semval = 0
        self.idx = 0
        self.raw = ()
        self.pos = 0


class Prog:
    ENGS = ("tensor", "vector", "scalar", "gpsimd", "sync")

    def __init__(self, nc, paranoid=False):
        self.nc = nc
        self.paranoid = paranoid
        self.q = {e: [] for e in self.ENGS}
        self.dma_keys = {}
        self.all = []

    SAME_ENG_WINDOW = 4

    def _same_eng_sync(self, ins, d):
        if d.eng != ins.eng or d.dma or ins.eng == "tensor" or ins.eng == "sync":
            return False
        return (d in ins.raw) and (ins.pos - d.pos) <= self.SAME_ENG_WINDOW

    def _track(self, ins, reads, writes):
        deps = set()
        for b in reads:
            if b.w is not None:
                deps.add(b.w)
        ins.raw = frozenset(deps)
        for b in writes:
            if b.w is not None:
                deps.add(b.w)
            for r in b.r:
                deps.add(r)
        deps.discard(ins)
        ins.deps = list(deps)
        for b in reads:
            b.r.append(ins)
        for b in writes:
            b.w = ins
            b.r = []

    def op(self, eng, fn, reads=(), writes=()):
        ins = Ins(eng, fn)
        ins.idx = len(self.all)
        ins.pos = len(self.q[eng])
        self.all.append(ins)
        self._track(ins, reads, writes)
        self.q[eng].append(ins)
        return ins

    def dma(self, eng, fn, reads=(), writes=(), key=None):
        ins = Ins(eng, fn, dma=True)
        ins.idx = len(self.all)
        ins.pos = len(self.q[eng])
        self.all.append(ins)
        self._track(ins, reads, writes)
        st = self.dma_keys.setdefault(key, [0])
        st[0] += 16
        ins.sem = key
        ins.semval = st[0]
        self.q[eng].append(ins)
        return ins

    def emit(self, final_waits=()):
        nc = self.nc
        for ins in self.all:
            for d in ins.deps:
                if d.dma:
                    continue
                if d.eng == ins.eng and not self._same_eng_sync(ins, d):
                    continue
                d.sig = True
        for e in self.ENGS:
            c = 0
            for ins in self.q[e]:
                if ins.dma:
                    continue
                if ins.sig:
                    c += 1
                ins.cnt = c
        with ExitStack() as es:
            esem = {e: es.enter_context(nc.semaphore("es_" + e)) for e in self.ENGS}
            dsem = {k: es.enter_context(nc.semaphore("ds_%d" % i)) for i, k in enumerate(self.dma_keys)}
            block = es.enter_context(nc.Block())
            stats = {}

            def run(e, h):
                waited = {}
                nw = 0
                for ins in self.q[e]:
                    need = {}
                    for d in ins.deps:
                        if d.dma:
                            s, v = dsem[d.sem], d.semval
                        else:
                            if d.eng == e and not self._same_eng_sync(ins, d):
                                continue
                            s, v = esem[d.eng], d.cnt
                        if waited.get(id(s), 0) >= v:
                            continue
                        if need.get(id(s), (None, 0))[1] < v:
                            need[id(s)] = (s, v)
                    for sid, (s, v) in need.items():
                        h.wait_ge(s, v)
                        waited[sid] = v
                        nw += 1
                    r = ins.fn(h)
                    if ins.dma:
                        r.then_inc(dsem[ins.sem], 16)
                    elif ins.sig:
                        r.then_inc(esem[e], 1)
                for (kind, key_or_ins) in (final_waits if e == "sync" else ()):
                    if kind == "dma":
                        h.wait_ge(dsem[key_or_ins], self.dma_keys[key_or_ins][0])
                stats[e] = (len(self.q[e]), nw)

            @block.tensor
            def _(h):
                run("tensor", h)

            @block.vector
            def _(h):
                run("vector", h)

            @block.scalar
            def _(h):
                run("scalar", h)

            @block.gpsimd
            def _(h):
                run("gpsimd", h)

            @block.sync
            def _(h):
                run("sync", h)
        self.stats = stats
        return stats

NPAR = 208
C_ONES, C_IDENT, C_TRIU, C_NEG, C_P64, C_P32, C_MSC, C_MSP = [i * 128 for i in range(8)]
C_MD = 1024
C_RM = 1024 + 2048
C_HM = C_RM + 4
NCST = C_HM + 4

O_MQ, O_MK, O_MV, O_MO, O_MI, O_MF, O_SQ, O_SK, O_SV, O_DQ, O_DK, O_DV = 0, 256, 512, 768, 1024, 1028, 1032, 1544, 1672, 1800, 2056, 2312
FM_SQ, FM_SK, FM_MO, TMB0, NWB = 512, 1024, 1152, 1408, 1800
A_DQ, A_DK, A_DV, NWA = 0, 256, 512, 768


def _perm_in():
    cols = []
    cols += list(range(O_MQ, O_MQ + 256)) + list(range(O_MK, O_MK + 256))
    for c in range(4):
        cols += list(range(O_SQ + c * 64, O_SQ + c * 64 + 64)) + list(range(O_SQ + (4 + c) * 64, O_SQ + (4 + c) * 64 + 64))
    cols += list(range(O_SK, O_SK + 128))
    cols += list(range(O_MO, O_MO + 256))
    cols += list(range(O_MV, O_MV + 256))
    cols += list(range(O_SV, O_SV + 128)) + list(range(O_MI, O_MI + 4)) + list(range(O_MF, O_MF + 4))
    cols += list(range(O_DQ, O_DQ + 256)) + list(range(O_DK, O_DK + 256)) + list(range(O_DV, O_DV + 256))
    assert len(cols) == INW and len(set(cols)) == INW
    return np.array(cols)


def _consts(S):
    c = np.zeros((128, NCST), np.float32)
    idx = np.arange(128)
    c[:, C_ONES:C_ONES + 128] = 1.0
    c[:, C_IDENT:C_IDENT + 128] = np.eye(128, dtype=np.float32)
    c[:, C_TRIU:C_TRIU + 128] = (idx[:, None] <= idx[None, :])
    c[:, C_NEG:C_NEG + 128] = np.where(idx[:, None] <= idx[None, :], 0.0, -30000.0)
    for (off, hd) in ((C_P64, 64), (C_P32, 32)):
        half = hd // 2
        d = idx % hd
        partner = np.where(d < half, idx + half, idx - half)
        pm = np.zeros((128, 128), np.float32)
        pm[partner, idx] = 1.0
        c[:, off:off + 128] = pm
    c[:, C_MSC:C_MSC + 128] = (idx[:, None] <= idx[None, :])
    c[:, C_MSP:C_MSP + 128] = (idx[:, None] > idx[None, :])
    q = np.arange(512)
    for o in range(4):
        c[:, C_MD + o * 512:C_MD + (o + 1) * 512] = ((o * 128 + idx)[:, None] <= q[None, :])
    for j in range(4):
        c[32 * j:32 * j + 32, C_RM + j] = 1.0
    c[0:64, C_HM] = 1.0
    c[64:128, C_HM + 1] = 1.0
    tabs = np.zeros((4, 128, S), np.float32)
    pos = np.arange(S).astype(np.float32)
    for ti, hd in ((0, 64), (2, 32)):
        half = hd // 2
        inv = (np.float32(10000.0) ** (-np.arange(half, dtype=np.float32) * np.float32(2.0) / np.float32(hd))).astype(np.float32)
        ang = (pos[:, None] * inv[None, :]).astype(np.float32)
        d = idx % hd
        fi = d % half
        sign = np.where(d < half, -1.0, 1.0).astype(np.float32)
        tabs[ti] = np.cos(ang)[:, fi].T
        tabs[ti + 1] = (np.sin(ang)[:, fi] * sign[None, :]).T
    return c, tabs


def _params(inp, l):
    p = np.zeros((128, NPAR), np.float32)
    fm = lambda v: np.asarray(v, np.float32).reshape(-1, 128).T
    p[:, 0:8] = fm(inp['g_pre_mix'][l])
    p[:, 8:16] = fm(inp['g_post_mix'][l])
    p[:, 16:24] = fm(inp['g_pre_mlp'][l])
    p[:, 24:32] = fm(inp['g_post_mlp'][l])
    cw = np.asarray(inp['conv_w'][l], np.float32)
    p[:, 32:48] = cw.reshape(4, 4, 128).transpose(2, 1, 0).reshape(128, 16)
    p[:, 48:52] = fm(inp['conv_b'][l])
    p[0:64, 52:56] = np.asarray(inp['m_norm_g'][l], np.float32).reshape(4, 64).T
    p[0:64, 56] = np.asarray(inp['sub_g'][l], np.float32)
    p[:, 57:61] = np.asarray(inp['i_bias'][l], np.float32)[None, :]
    p[:, 61:65] = np.asarray(inp['f_bias'][l], np.float32)[None, :]
    p[:, 65:73] = np.asarray(inp['sinks'][l], np.float32)[None, :]
    for k, nm in enumerate(('lam_q1', 'lam_k1', 'lam_q2', 'lam_k2')):
        p[:, 73 + 32 * k:73 + 32 * (k + 1)] = np.asarray(inp[nm][l], np.float32)[None, :]
    return p


class Arena:
    def __init__(self, nc, es, words):
        self.t = es.enter_context(nc.sbuf_tensor("arena", [128, words], F32))
        self.dry = False
        self.words = words
        self.off = 0
        self.bufs = []
        self.pending = []
        self.peak = 0

    def reset(self):
        s = set(self.pending)
        for b in self.bufs:
            if b.w is not None:
                s.add(b.w)
            s.update(b.r)
        latest = {}
        keep = []
        for ins in s:
            if ins.dma:
                keep.append(ins)
            else:
                if ins.eng not in latest or latest[ins.eng].idx < ins.idx:
                    latest[ins.eng] = ins
        self.pending = keep + list(latest.values())
        self.off = 0
        self.bufs = []

    def tile(self, free, dtype, parts=128):
        n = int(np.prod(free))
        w = n if dtype == F32 else (n + 1) // 2
        w = (w + 7) // 8 * 8
        if not self.dry:
            a = self.t[:, self.off:self.off + w]
        self.off += w
        self.peak = max(self.peak, self.off)
        if self.dry:
            return None
        assert self.off <= self.words, ("SBUF arena overflow", self.off, self.words)
        if dtype != F32:
            a = a.bitcast(dtype)
        a = a[:, 0:n]
        if len(free) == 2:
            a = a.rearrange("p (a b) -> p a b", a=free[0])
        elif len(free) == 3:
            a = a.rearrange("p (a b c) -> p a b c", a=free[0], b=free[1])
        elif len(free) == 4:
            a = a.rearrange("p (a b c d) -> p a b c d", a=free[0], b=free[1], c=free[2])
        return a

    def buf(self, name):
        b = Buf(name)
        b.r = list(self.pending)
        self.bufs.append(b)
        return b


import os as _os
MBSTOP = int(_os.environ.get('MBSTOP', '99'))
MLSTOP = int(_os.environ.get('MLSTOP', '99'))
VARX = _os.environ.get('VARX', '')


def build(S=SEQ, L=DEPTH, phases=None, arena_kb=196, dbg=False, dry=False):
    NT = S // 512
    NF = S // 256
    nc = bass.Bass("TRN2", target_bir_lowering=False)

    def dt(n, s, k="ExternalInput"):
        return nc.dram_tensor(n, s, F32, kind=k).ap()
    xT = dt("xT", [D, S])
    w_in = dt("w_in", [L, D, INW])
    w_out = dt("w_out", [L, D, D])
    w_up = dt("w_up", [L, D, DFF])
    w_down = dt("w_down", [L, DFF, D])
    par = dt("par", [L, 128, NPAR])
    cst = dt("cst", [128, NCST])
    tab = dt("tab", [4, 128, S])
    yT = dt("yT", [D, S], "ExternalOutput")
    sA = dt("sA", [D, S], "ExternalOutput" if dbg else "Internal")
    sB = dt("sB", [D, S], "Internal")
    sD = nc.dram_tensor("sD", [4, 64, S], BF16, kind="Internal").ap()
    dmix = nc.dram_tensor("dmix", [128, 8, S], BF16, kind="ExternalOutput").ap() if dbg else None
    dbufs = {}

    def dbuf(name, j):
        k = (name, j)
        if k not in dbufs:
            dbufs[k] = Buf("d_%s_%d" % k)
        return dbufs[k]

    P = Prog(nc)

    def MM(out, lhsT, rhs, start, stop, r, w):
        P.op("tensor", lambda h: h.matmul(out, lhsT=lhsT, rhs=rhs, start=start, stop=stop), r, w)

    def ACT(out, in_, func, r, w, bias=None, scale=None):
        kw = {}
        if bias is not None:
            kw['bias'] = bias
        if scale is not None:
            kw['scale'] = scale
        P.op("scalar", lambda h: h.activation(out=out, in_=in_, func=func, **kw), r, w)

    def TT(eng, out, in0, in1, op, r, w):
        P.op(eng, lambda h: h.tensor_tensor(out=out, in0=in0, in1=in1, op=op), r, w)

    def TS(eng, out, in0, s1, s2, op0, op1, r, w):
        if op1 is None:
            P.op(eng, lambda h: h.tensor_scalar(out=out, in0=in0, scalar1=s1, scalar2=None, op0=op0), r, w)
        else:
            P.op(eng, lambda h: h.tensor_scalar(out=out, in0=in0, scalar1=s1, scalar2=s2, op0=op0, op1=op1), r, w)

    def STT(out, in0, sc, in1, op0, op1, r, w):
        P.op("vector", lambda h: h.scalar_tensor_tensor(out=out, in0=in0, scalar=sc, in1=in1, op0=op0, op1=op1), r, w)

    def CP(eng, out, in_, r, w):
        P.op(eng, lambda h: h.tensor_copy(out=out, in_=in_), r, w)

    def RCP(out, in_, r, w):
        P.op("vector", lambda h: h.reciprocal(out=out, in_=in_), r, w)

    def MSET(eng, ap, val, w):
        P.op(eng, lambda h: h.memset(ap, val), (), w)

    dcount = [0]

    def DMA(out, in_, r, w, key=None):
        if key is None:
            key = "k%d" % dcount[0]
            dcount[0] += 1
        P.dma("sync", lambda h: h.dma_start(out=out, in_=in_), r, w, key)
        return key

    final_keys = []
    with ExitStack() as es:
        sb = lambda n, s, d: es.enter_context(nc.sbuf_tensor(n, s, d))
        cf = sb("cf", [128, 4, 128], F32)
        cm = sb("cm", [128, 8], F32)
        cb = sb("cb", [128, 6, 128], BF16)
        cmd = sb("cmd", [128, 4, 512], BF16)
        prm = sb("prm", [128, L, NPAR], F32)
        epsT = sb("epsT", [128, 1], F32)
        lamT = sb("lamT", [128, L, 4], F32)
        sgl = sb("sgl", [128, L], F32)
        esk = sb("esk", [128, L, 8], F32)
        pst = [es.enter_context(nc.psum_tensor("ps%d" % i, [128, 512], F32)) for i in range(8)]
        pb = [Buf("ps%d" % i) for i in range(8)]
        pb7 = [Buf("ps7_%d" % i) for i in range(4)]
        A = Arena(nc, es, arena_kb * 256)
        bC = Buf("consts")

        stg = A.tile([3080], F32)
        bS = A.buf("cstg")
        DMA(stg, cst[:, 0:3080], [], [bS])
        DMA(prm[:], par.rearrange("l p n -> p l n"), [], [bC])
        CP("vector", cf[:, 0, :], stg[:, C_ONES:C_ONES + 128], [bS], [bC])
        CP("vector", cf[:, 1, :], stg[:, C_IDENT:C_IDENT + 128], [bS], [bC])
        CP("vector", cf[:, 2, :], stg[:, C_TRIU:C_TRIU + 128], [bS], [bC])
        CP("vector", cf[:, 3, :], stg[:, C_NEG:C_NEG + 128], [bS], [bC])
        CP("vector", cm[:, 0:6], stg[:, C_RM:C_RM + 6], [bS], [bC])
        for k, off in enumerate((C_ONES, C_IDENT, C_P64, C_P32, C_MSC, C_MSP)):
            CP("vector", cb[:, k, :], stg[:, off:off + 128], [bS], [bC])
        TS("vector", cmd[:].rearrange("p a b -> p (a b)"), stg[:, C_MD:C_MD + 2048], 30000.0, -30000.0, ALU.mult, ALU.add, [bS], [bC])
        MSET("vector", epsT[:], EPS, [bC])
        ONESF, IDF, TRIU, NEGM = cf[:, 0, :], cf[:, 1, :], cf[:, 2, :], cf[:, 3, :]
        ONESB, IDB, P64B, P32B, MSC, MSP = [cb[:, k, :] for k in range(6)]
        for l in range(L):
            lam_init = 0.8 - 0.6 * math.exp(-0.3 * l)
            tmp = A.tile([64], F32)
            bt = A.buf("lamtmp")
            for k in range(2):
                TT("vector", tmp[:, 32 * k:32 * k + 32], prm[:, l, 73 + 64 * k:73 + 64 * k + 32], prm[:, l, 105 + 64 * k:105 + 64 * k + 32], ALU.mult, [bC], [bt])
                P.op("vector", lambda h, o=lamT[:, l, 2 + k:3 + k], i=tmp[:, 32 * k:32 * k + 32]: h.reduce_sum(out=o, in_=i, axis=mybir.AxisListType.X), [bt], [bC])
            ACT(lamT[:, l, 2:4], lamT[:, l, 2:4], AF.Exp, [bC], [bC])
            TT("vector", lamT[:, l, 0:1], lamT[:, l, 2:3], lamT[:, l, 3:4], ALU.subtract, [bC], [bC])
            TS("vector", lamT[:, l, 0:1], lamT[:, l, 0:1], lam_init, None, ALU.add, None, [bC], [bC])
            TS("vector", lamT[:, l, 1:2], lamT[:, l, 0:1], -1.0, None, ALU.mult, None, [bC], [bC])
            TS("vector", sgl[:, l:l + 1], prm[:, l, 56:57], 1.0 - lam_init, None, ALU.mult, None, [bC], [bC])
            ACT(esk[:, l, :], prm[:, l, 65:73], AF.Exp, [bC], [bC])

        def load_weight(dst, src_rows, ncols, scale_ap, bW, stage, bSt, cnt):
            for c0 in range(0, ncols, 2048):
                c1 = min(ncols, c0 + 2048)
                s = cnt[0] % 2
                cnt[0] += 1
                DMA(stage[s][:, 0:c1 - c0], src_rows[:, c0:c1], [], [bSt[s]], key="wst%d" % s)
                if s == 1:
                    ACT(dst[:, c0:c1], stage[s][:, 0:c1 - c0], AF.Copy, [bSt[s], bC], [bW], scale=scale_ap)
                elif scale_ap is None:
                    CP("vector", dst[:, c0:c1], stage[s][:, 0:c1 - c0], [bSt[s]], [bW])
                else:
                    TS("vector", dst[:, c0:c1], stage[s][:, 0:c1 - c0], scale_ap, None, ALU.mult, None, [bSt[s], bC], [bW])

        def norm_rstd(ss_ps, bps, nparts, inv_n, rs_out, brs, tmp, btmp):
            ACT(tmp, ss_ps, AF.Ln, [bps, bC], [btmp], bias=epsT[0:nparts, :], scale=inv_n)
            ACT(rs_out, tmp, AF.Exp, [btmp], [brs], scale=-0.5)

        for l in range(L):
            src = xT if l == 0 else sB
            srcn = "xT" if l == 0 else "sB"
            dst_f = yT if l == L - 1 else sB
            dstn_f = "yT" if l == L - 1 else "sB"
            PR = lambda a, b, _l=l: prm[:, _l, a:b]
            if phases is None or "mix" in phases or "mixA" in phases or "mixB" in phases:
                A.reset()
                win = A.tile([8, NWA], BF16)
                bWin = A.buf("winA")
                KTd = A.tile([2, S], BF16)
                bKTdT = [A.buf("KTd%d" % k) for k in range(NT)]
                Vd = A.tile([S // 128, 4, 128], BF16)
                bVdT = [A.buf("Vd%d" % k) for k in range(NT)]
                xt = A.tile([8, 512], F32)
                bX = A.buf("x")
                qdp_raw = A.tile([2, 2048], F32)
                Qdp2 = [qdp_raw[:, k, :].bitcast(BF16).rearrange("p (a b) -> p a b", a=8) for k in range(2)]
                bQdp2 = [A.buf("Qdp0"), A.buf("Qdp1")]
                stage = [xt.rearrange("p a b -> p (a b)")[:, 0:2048], qdp_raw[:, 0, :]]
                bSt = [bX, bQdp2[0]]
                hT = A.tile([8, 512], BF16)
                bH = A.buf("hT")
                sq = A.tile([8, 512], BF16)
                bSq = A.buf("sq")
                rs = A.tile([512], F32)
                bRs = A.buf("rs")
                tm1 = A.tile([512], F32)
                bT1 = A.buf("tm1")
                tbl = A.tile([2, 512], F32)
                bTb = A.buf("tbl")
                qb = A.tile([512], BF16)
                bQb = A.buf("qb")
                t1 = A.tile([512], F32)
                t2 = A.tile([512], F32)
                bT1r, bT2r = A.buf("t1"), A.buf("t2")
                qr = A.tile([512], BF16)
                bQr = A.buf("qr")
                rd = A.tile([2, 512], F32)
                bRd = [A.buf("rd0"), A.buf("rd1")]
                sqh = A.tile([512], BF16)
                bSqh = A.buf("sqh")
                dpc = A.tile([4, 512], BF16)
                bDpc = A.buf("dpc")
                NPT = 6
                SCB = [2, 3, 7]
                Pt = A.tile([NPT, 512], BF16)
                bPt = [A.buf("Pt%d" % k) for k in range(NPT)]
                Aj = A.tile([2, 512], F32)
                bAj = [A.buf("Aj0"), A.buf("Aj1")]
                Dh = A.tile([512], F32)
                bDh = A.buf("Dh")
                cnt = [0]
                for kc in range(8):
                    load_weight(win[:, kc, :], w_in[l, kc * 128:(kc + 1) * 128, NWB:NWB + NWA], NWA, PR(kc, kc + 1), bWin, stage, bSt, cnt)
                MSET("gpsimd", Vd[:, :, :, 64:128], 1.0, bVdT)
                gen = [0]
                scn = [0]
                oac = [0]
                ptc = [0]

                def gen_bank():
                    k = gen[0] % 2
                    gen[0] += 1
                    return pst[k], pb[k]

                def sc_bank():
                    k = 2 + scn[0] % 2
                    scn[0] += 1
                    return pst[k], pb[k]

                def o_bank():
                    k = 4 + oac[0] % 2
                    oac[0] += 1
                    return pst[k], pb[k]

                def prologue(i, xt, bX, tbl, bTb, tsl, hT, bH, sq, bSq, rs, bRs, tm1, bT1):
                    t0 = i * 512
                    DMA(xt, src.rearrange("(k p) t -> p k t", p=128)[:, :, t0:t0 + 512],
                        [dbuf(srcn, 2 * i), dbuf(srcn, 2 * i + 1)], [bX], key="xld")
                    DMA(tbl, tab.rearrange("f p t -> p f t")[:, tsl, t0:t0 + 512], [], [bTb], key="tbl")
                    ACT(sq.rearrange("p a b -> p (a b)"), xt.rearrange("p a b -> p (a b)"), AF.Square, [bX], [bSq])
                    for kc in range(8):
                        MM(pst[6][:, :], ONESB, sq[:, kc, :], kc == 0, kc == 7, [bSq, bC], [pb[6]])
                    norm_rstd(pst[6][:, :], pb[6], 128, 1.0 / D, rs, bRs, tm1, bT1)
                    TT("vector", hT[:, 0:6, :], xt[:, 0:6, :], rs.unsqueeze(1).broadcast_to([128, 6, 512]), ALU.mult, [bX, bRs], [bH])
                    TT("gpsimd", hT[:, 6:8, :], xt[:, 6:8, :], rs.unsqueeze(1).broadcast_to([128, 2, 512]), ALU.mult, [bX, bRs], [bH])

                def inproj_fm(col0, m=128):
                    pt, pbuf = gen_bank()
                    for kc in range(8):
                        MM(pt[0:m, :], win[:, kc, col0:col0 + m], hT[:, kc, :], kc == 0, kc == 7, [bWin, bH], [pbuf])
                    return pt, pbuf

                def rope_chunk(col0, permb, dest, bdest):
                    pt, pbuf = inproj_fm(col0)
                    CP("vector", qb, pt[:, :], [pbuf], [bQb])
                    MM(pst[6][:, :], permb, qb, True, True, [bQb, bC], [pb[6]])
                    TT("vector", t1, pt[:, :], tbl[:, 0, :], ALU.mult, [pbuf, bTb], [bT1r])
                    TT("vector", t2, pst[6][:, :], tbl[:, 1, :], ALU.mult, [pb[6], bTb], [bT2r])
                    TT("vector", dest, t1, t2, ALU.add, [bT1r, bT2r], [bdest])

                def f_units(i):
                    t0 = i * 512
                    qs = i % 2
                    us = []
                    us.append(lambda: prologue(i, xt, bX, tbl, bTb, slice(2, 4), hT, bH, sq, bSq, rs, bRs, tm1, bT1))

                    def uq(c):
                        rope_chunk(A_DQ + c * 128, P32B, qr, bQr)
                        for jj in range(4):
                            TS("vector", Qdp2[qs][:, 4 * c + jj, :], qr, cm[:, jj:jj + 1], None, ALU.mult, None, [bQr, bC], [bQdp2[qs]])

                    def uk(c):
                        rope_chunk(A_DK + c * 128, P32B, KTd[:, c, t0:t0 + 512], bKTdT[i])

                    def uv(bi):
                        pt, pbuf = gen_bank()
                        for kc in range(8):
                            MM(pt[:, 0:256], hT[:, kc, bi * 128:(bi + 1) * 128], win[:, kc, A_DV:A_DV + 256], kc == 0, kc == 7, [bWin, bH], [pbuf])
                        CP("vector", Vd[:, 4 * i + bi, :, 0:64], pt[:, 0:256].rearrange("p (h d) -> p h d", h=4), [pbuf], [bVdT[i]])
                    for c in range(2):
                        us.append(lambda c=c: uq(c))
                    for c in range(2):
                        us.append(lambda c=c: uk(c))
                    for bi in range(4):
                        us.append(lambda bi=bi: uv(bi))
                    return us

                NTA = NT if (phases is None or 'mix' in phases or 'mixA' in phases) else 0
                if NTA:
                    for u in f_units(0):
                        u()
                for i in range(NTA):
                    t0 = i * 512
                    Qdp = Qdp2[i % 2]
                    bQdp = bQdp2[i % 2]
                    nxt = f_units(i + 1) if i + 1 < NTA else []
                    nxt_all = list(nxt)
                    nkb = 4 * (i + 1)
                    LAG = 2
                    n_items = 8 * nkb
                    item = [0]
                    pend = []
                    deferred = []

                    def tick():
                        for d_ in deferred:
                            d_[0] -= 1
                        while deferred and deferred[0][0] <= 0:
                            deferred.pop(0)[1]()

                    def fin_a(j, ot, obuf):
                        h4 = j // 2
                        jp = j % 2
                        RCP(rd[0:64, jp, :], ot[64:128, :], [obuf], [bRd[jp]])
                        TT("vector", Aj[0:64, jp, :], ot[0:64, :], rd[0:64, jp, :], ALU.mult, [obuf, bRd[jp]], [bAj[jp]])
                        if jp == 1:
                            STT(Dh[0:64, :], Aj[0:64, 1, :], lamT[0:64, l, 1:2], Aj[0:64, 0, :], ALU.mult, ALU.add, [bAj[0], bAj[1], bC], [bDh])
                            TT("vector", sqh[0:64, :], Dh[0:64, :], Dh[0:64, :], ALU.mult, [bDh], [bSqh])

                            def part_b(h4=h4):
                                MM(pst[6][0:64, :], ONESB[0:64, 0:64], sqh[0:64, :], True, True, [bSqh, bC], [pb[6]])
                                norm_rstd(pst[6][0:64, :], pb[6], 64, 1.0 / 64, rs[0:64, :], bRs, tm1[0:64, :], bT1)
                                STT(dpc[0:64, h4, :], Dh[0:64, :], sgl[0:64, l:l + 1], rs[0:64, :], ALU.mult, ALU.mult, [bDh, bRs, bC], [bDpc])
                            deferred.append([5, part_b])

                    def stage2(j, kb, k3, ot, obuf):
                        MM(ot[:, :], Vd[:, kb, j // 2, :], Pt[:, k3, :], kb == 0, kb == nkb - 1, [bVdT[kb // 4], bPt[k3]], [obuf])
                        if kb == nkb - 1:
                            fin_a(j, ot, obuf)

                    for j in range(8):
                        c = j // 4
                        ot, obuf = o_bank()
                        for kb in range(nkb):
                            sk_ = SCB[scn[0] % len(SCB)]
                            scn[0] += 1
                            st, sbuf_ = pst[sk_], pb[sk_]
                            diag = kb >= 4 * i
                            MM(st[:, :], KTd[:, c, kb * 128:(kb + 1) * 128], Qdp[:, j, :], True, not diag, [bKTdT[kb // 4], bQdp], [sbuf_])
                            if diag:
                                MM(st[:, :], IDB, cmd[:, kb - 4 * i, :], False, True, [bC], [sbuf_])
                            k3 = ptc[0] % NPT
                            ptc[0] += 1
                            ACT(Pt[:, k3, :], st[:, :], AF.Exp, [sbuf_], [bPt[k3]], scale=32.0 ** -0.5)
                            pend.append((j, kb, k3, ot, obuf))
                            if len(pend) > LAG:
                                stage2(*pend.pop(0))
                            tick()
                            item[0] += 1
                            if nxt and item[0] * (len(nxt_all) + 1) >= (len(nxt_all) - len(nxt) + 1) * n_items:
                                nxt.pop(0)()
                    while pend:
                        stage2(*pend.pop(0))
                    while deferred:
                        deferred.pop(0)[1]()
                    while nxt:
                        nxt.pop(0)()
                    DMA(sD[:, :, t0:t0 + 512].rearrange("h p t -> p h t"), dpc[0:64, :, :], [bDpc], [dbuf("sD", i)], key="dst")

            if phases is None or "mix" in phases or "mixB" in phases:
                A.reset()
                win = A.tile([8, NWB], BF16)
                wout = A.tile([8, D], BF16)
                bWin, bWout = A.buf("win"), A.buf("wout")
                xt = A.tile([8, 512], F32)
                bX = A.buf("x")
                yv = A.tile([8, 512], F32)
                bY = A.buf("y")
                stage = [xt.rearrange("p a b -> p (a b)")[:, 0:2048], yv.rearrange("p a b -> p (a b)")[:, 0:2048]]
                bSt = [bX, bY]
                hT = A.tile([8, 512], BF16)
                bH = A.buf("hT")
                sq = A.tile([8, 512], BF16)
                bSq = A.buf("sq")
                rs = A.tile([512], F32)
                bRs = A.buf("rs")
                tm1 = A.tile([512], F32)
                bT1 = A.buf("tm1")
                tbl = A.tile([2, 512], F32)
                bTb = A.buf("tbl")
                pre = A.tile([515], F32)
                halo = A.tile([4, 3], F32)
                bPre = A.buf("pre")
                bHalo = A.buf("halo")
                acc = A.tile([512], F32)
                bAcc = A.buf("acc")
                qm = A.tile([2, 512], BF16)
                km = A.tile([2, 512], BF16)
                bQm, bKm = A.buf("qm"), A.buf("km")
                qmp = A.tile([4, 512], BF16)
                bQmp = A.buf("qmp")
                og = A.tile([4, 512], BF16)
                bOg = A.buf("og")
                qb = A.tile([512], BF16)
                bQb = A.buf("qb")
                t1 = A.tile([512], F32)
                t2 = A.tile([512], F32)
                bT1r, bT2r = A.buf("t1"), A.buf("t2")
                qr = A.tile([512], BF16)
                bQr = A.buf("qr")
                QTs = A.tile([4, 512], BF16)
                bQTs = A.buf("QTs")
                KTs = A.tile([2, 640], BF16)
                bKTs = A.buf("KTs")
                Vs = A.tile([5, 2, 128], BF16)
                bVs = A.buf("Vs")
                Vm = A.tile([4, 4, 128], BF16)
                bVm = A.buf("Vm")
                gt = A.tile([4, 8], F32)
                bGt = A.buf("gt")
                lft = A.tile([4, 4], F32)
                bLf = A.buf("lft")
                gtmp = A.tile([4, 4], F32)
                bGtmp = A.buf("gtmp")
                u4d = A.tile([2, 4], F32)
                w4d = A.tile([2, 4], F32)
                decd = A.tile([2, 4], F32)
                LFbd = A.tile([2, 4, 128], F32)
                EBd = A.tile([2, 4, 128], F32)
                DTd = A.tile([2, 4, 128], F32)
                scd = A.tile([2, 4, 128], BF16)
                Qsd = A.tile([2, 4, 128], BF16)
                mlb = {nm: [A.buf(nm + "0"), A.buf(nm + "1")] for nm in ("u4", "w4", "dec", "LFb", "EB", "DT", "sc", "Qs")}
                Kw4 = A.tile([256], BF16)
                bKw = A.buf("Kw")
                C32 = A.tile([4, 128], F32)
                Cbf = A.tile([4, 128], BF16)
                bC32A = A.buf("C32")
                bCbfA = A.buf("Cbf")
                dn = A.tile([2, 512], F32)
                rd = A.tile([512], F32)
                bDn, bRd = A.buf("dn"), A.buf("rd")
                hmt = A.tile([4, 512], F32)
                bHm = [A.buf("hm%d" % h) for h in range(4)]
                sqh = A.tile([512], BF16)
                bSqh = A.buf("sqh")
                tmph = A.tile([512], F32)
                bTmph = A.buf("tmph")
                pieceb = A.tile([512], BF16)
                bPiece = A.buf("piece")
                mix = A.tile([8, 512], BF16)
                bMix = A.buf("mix")
                bMixD = A.buf("mixD")
                NPT = 4
                Pt = A.tile([NPT, 512], BF16)
                bPt = [A.buf("Pt%d" % k) for k in range(NPT)]
                cnt = [0]
                for kc in range(8):
                    load_weight(win[:, kc, :], w_in[l, kc * 128:(kc + 1) * 128, 0:NWB], NWB, PR(kc, kc + 1), bWin, stage, bSt, cnt)
                for kc in range(8):
                    load_weight(wout[:, kc, :], w_out[l, kc * 128:(kc + 1) * 128, :], D, None, bWout, stage, bSt, cnt)
                nmS = A.tile([2, 512], BF16)
                bNmS = A.buf("nmS")
                for m_, msk_ in enumerate((MSP, MSC)):
                    TS("vector", nmS[:, m_, :].rearrange("p (h q) -> p h q", h=4), msk_.unsqueeze(1).broadcast_to([128, 4, 128]), 30000.0, -30000.0, ALU.mult, ALU.add, [bC], [bNmS])
                MSET("gpsimd", halo[:], 0.0, [bHalo])
                MSET("gpsimd", KTs[:], 0.0, [bKTs])
                MSET("gpsimd", Vs[:], 0.0, [bVs])
                MSET("gpsimd", Vs[:, :, :, 64:128], 1.0, [bVs])
                MSET("gpsimd", Vm[:, :, :, 64:128], 1.0, [bVm])
                MSET("vector", C32[:], 0.0, [bC32A])
                MSET("vector", Cbf[:], 0.0, [bCbfA])
                gen = [0]
                scn = [0]
                oac = [0]
                ptc = [0]

                for i in range(NT):
                    t0 = i * 512
                    prologue(i, xt, bX, tbl, bTb, slice(0, 2), hT, bH, sq, bSq, rs, bRs, tm1, bT1)
                    for h4 in range(4):
                        DMA(mix[(h4 % 2) * 64:(h4 % 2) * 64 + 64, 6 + h4 // 2, :], sD[h4, :, t0:t0 + 512], [dbuf("sD", i)], [bMixD], key="dld")
                    if MBSTOP < 1:
                        continue
                    for c in range(4):
                        pt, pbuf = inproj_fm(c * 128)
                        CP("vector", pre[:, 0:3], halo[:, c, :], [bHalo], [bPre])
                        CP("vector", pre[:, 3:515], pt[:, :], [pbuf], [bPre])
                        CP("vector", halo[:, c, :], pre[:, 512:515], [bPre], [bHalo])
                        TS("vector", acc, pre[:, 3:515], PR(32 + c * 4 + 3, 32 + c * 4 + 4), PR(48 + c, 49 + c), ALU.mult, ALU.add, [bPre, bC], [bAcc])
                        for j in (2, 1, 0):
                            STT(acc, pre[:, j:j + 512], PR(32 + c * 4 + j, 32 + c * 4 + j + 1), acc, ALU.mult, ALU.add, [bPre, bAcc, bC], [bAcc])
                        if c < 2:
                            ACT(qm[:, c, :], acc, AF.Silu, [bAcc], [bQm])
                            for hh in range(2):
                                TS("vector", qmp[:, 2 * c + hh, :], qm[:, c, :], cm[:, 4 + hh:5 + hh], None, ALU.mult, None, [bQm, bC], [bQmp])
                        else:
                            ACT(t1, acc, AF.Silu, [bAcc], [bT1r])
                            TS("vector", km[:, c - 2, :], t1, 0.125, None, ALU.mult, None, [bT1r], [bKm])
                    if MBSTOP < 2:
                        continue
                    for h4 in range(4):
                        pt, pbuf = inproj_fm(FM_MO + h4 * 64, m=64)
                        ACT(og[0:64, h4, :], pt[0:64, :], AF.Sigmoid, [pbuf], [bOg])
                    if MBSTOP < 3:
                        continue
                    for c in range(4):
                        rope_chunk(FM_SQ + c * 128, P64B, QTs[:, c, :], bQTs)
                    rope_chunk(FM_SK, P64B, qr, bQr)
                    for g in range(2):
                        TS("vector", KTs[:, g, 128:640], qr, cm[:, 4 + g:5 + g], None, ALU.mult, None, [bQr, bC], [bKTs])
                    if MBSTOP < 4:
                        continue
                    for bi in range(4):
                        pt, pbuf = gen_bank()
                        for kc in range(8):
                            MM(pt[:, 0:392], hT[:, kc, bi * 128:(bi + 1) * 128], win[:, kc, TMB0:TMB0 + 392], kc == 0, kc == 7, [bWin, bH], [pbuf])
                        CP("vector", Vm[:, bi, :, 0:64], pt[:, 0:256].rearrange("p (h d) -> p h d", h=4), [pbuf], [bVm])
                        CP("vector", Vs[:, 1 + bi, :, 0:64], pt[:, 256:384].rearrange("p (g d) -> p g d", g=2), [pbuf], [bVs])
                        TT("vector", gt[:, bi, :], pt[:, 384:392], PR(57, 65), ALU.add, [pbuf, bC], [bGt])
                    TS("vector", gtmp, gt[:, :, 4:8], -1.0, None, ALU.mult, None, [bGt], [bGtmp])
                    ACT(gtmp.rearrange("p a b -> p (a b)"), gtmp.rearrange("p a b -> p (a b)"), AF.Exp, [bGtmp], [bGtmp])
                    TS("vector", gtmp, gtmp, 1.0, None, ALU.add, None, [bGtmp], [bGtmp])
                    ACT(gtmp.rearrange("p a b -> p (a b)"), gtmp.rearrange("p a b -> p (a b)"), AF.Ln, [bGtmp], [bGtmp])
                    TS("vector", lft, gtmp, -1.0, None, ALU.mult, None, [bGtmp], [bLf])

                    def ml_s1(bi):
                        blk = slice(bi * 128, (bi + 1) * 128)
                        s_ = bi % 2
                        u4, w4, dec = u4d[:, s_, :], w4d[:, s_, :], decd[:, s_, :]
                        LFb4, EB4, DT4, sc4, Qs4 = LFbd[:, s_], EBd[:, s_], DTd[:, s_], scd[:, s_], Qsd[:, s_]
                        bU4, bW4, bDec, bLFb, bEB, bDT, bSc, bQs = [mlb[nm][s_] for nm in ("u4", "w4", "dec", "LFb", "EB", "DT", "sc", "Qs")]
                        MM(pst[6][:, 0:4], TRIU, lft[:, bi, :], True, True, [bLf, bC], [pb[6]])
                        TT("vector", u4, gt[:, bi, 0:4], pst[6][:, 0:4], ALU.subtract, [bGt, pb[6]], [bU4])
                        CP("vector", LFb4, lft[:, bi, :].unsqueeze(2).broadcast_to([128, 4, 128]), [bLf], [bLFb])
                        for h in range(4):
                            MM(pst[0][:, h * 128:(h + 1) * 128], LFb4[:, h, :], TRIU, True, True, [bLFb, bC], [pb[0]])
                        for h in range(4):
                            MM(pst[1][:, h * 128:(h + 1) * 128], LFb4[:, h, :], TRIU, True, False, [bLFb, bC], [pb[1]])
                            MM(pst[1][:, h * 128:(h + 1) * 128], IDF, NEGM, False, True, [bC], [pb[1]])
                        ACT(EB4.rearrange("p a b -> p (a b)"), pst[0][:, :], AF.Exp, [pb[0]], [bEB])
                        for h in range(4):
                            ACT(DT4[:, h, :], pst[1][:, h * 128:(h + 1) * 128], AF.Exp, [pb[1], bU4], [bDT], bias=u4[:, h:h + 1])
                        CP("vector", w4, DT4[:, :, 127], [bDT], [bW4])
                        CP("vector", dec, EB4[:, :, 127], [bEB], [bDec])
                        for h in range(4):
                            MM(pst[7][:, h * 128:(h + 1) * 128], km[:, h // 2, blk], qmp[:, h, blk], True, True, [bKm, bQmp], [pb[7]])
                        TT("vector", sc4.rearrange("p a b -> p (a b)"), pst[7][:, :], DT4.rearrange("p a b -> p (a b)"), ALU.mult, [pb[7], bDT], [bSc])
                        TT("vector", Qs4, qmp[:, :, blk], EB4, ALU.mult, [bQmp, bEB], [bQs])

                    def ml_s2(bi):
                        blk = slice(bi * 128, (bi + 1) * 128)
                        s_ = bi % 2
                        u4, w4, dec = u4d[:, s_, :], w4d[:, s_, :], decd[:, s_, :]
                        LFb4, EB4, DT4, sc4, Qs4 = LFbd[:, s_], EBd[:, s_], DTd[:, s_], scd[:, s_], Qsd[:, s_]
                        bU4, bW4, bDec, bLFb, bEB, bDT, bSc, bQs = [mlb[nm][s_] for nm in ("u4", "w4", "dec", "LFb", "EB", "DT", "sc", "Qs")]
                        for h in range(4):
                            MM(pst[6][:, h * 128:(h + 1) * 128], Vm[:, bi, h, :], sc4[:, h, :], True, False, [bVm, bSc], [pb[6]])
                            MM(pst[6][:, h * 128:(h + 1) * 128], Cbf[:, h, :], Qs4[:, h, :], False, True, [bCbfA, bQs], [pb[6]])
                        CP("vector", dn[0:64, 0, :], pst[6][64:128, :], [pb[6]], [bDn])
                        STT(dn[0:64, 1, :], dn[0:64, 0, :], -1.0, dn[0:64, 0, :], ALU.mult, ALU.max, [bDn], [bDn])
                        TS("vector", dn[0:64, 0, :], dn[0:64, 1, :], 1.0, None, ALU.max, None, [bDn], [bDn])
                        ACT(dn[0:64, 1, :], dn[0:64, 0, :], AF.Ln, [bDn], [bDn])
                        ACT(rd[0:64, :], dn[0:64, 1, :], AF.Exp, [bDn], [bRd], scale=-1.0)
                        TT("vector", hmt[0:64, :, blk], pst[6][0:64, :].rearrange("p (h t) -> p h t", h=4), rd[0:64, :].rearrange("p (h t) -> p h t", h=4),
                           ALU.mult, [pb[6], bRd], bHm)
                        for c in range(2):
                            MM(pst[7][:, c * 128:(c + 1) * 128], km[:, c, blk], IDB, True, True, [bKm, bC], [pb[7]])
                        TT("vector", Kw4.rearrange("p (h d) -> p h d", h=4), pst[7][:, 0:256].rearrange("p (h d) -> p h d", h=4),
                           w4.unsqueeze(2).broadcast_to([128, 4, 64]), ALU.mult, [pb[7], bW4], [bKw])
                        for h in range(4):
                            MM(pst[0][:, h * 128:(h + 1) * 128], Kw4[:, (h // 2) * 128:(h // 2 + 1) * 128], Vm[:, bi, h, :], True, True, [bKw, bVm], [pb[0]])
                        dC = pst[0][:, :].rearrange("p (h t) -> p h t", h=4)
                        for par in range(2):
                            r5 = slice(par * 64, par * 64 + 64)
                            TT("vector", C32[r5, par::2, :], C32[r5, par::2, :], dec[r5, par::2].unsqueeze(2).broadcast_to([64, 2, 128]), ALU.mult, [bC32A, bDec], [bC32A])
                            TT("vector", C32[r5, par::2, :], C32[r5, par::2, :], dC[r5, par::2, :], ALU.add, [bC32A, pb[0]], [bC32A])
                            CP("vector", Cbf[r5, par::2, :], C32[r5, par::2, :], [bC32A], [bCbfA])
                        hmb = hmt[0:64, :, blk]
                        v4 = lambda ap: ap.rearrange("p (h t) -> p h t", h=4)
                        ACT(v4(sqh[0:64, :]), hmb, AF.Square, bHm, [bSqh])
                        MM(pst[6][0:64, :], ONESB[0:64, 0:64], sqh[0:64, :], True, True, [bSqh, bC], [pb[6]])
                        norm_rstd(pst[6][0:64, :], pb[6], 64, 1.0 / 64, rs[0:64, :], bRs, tm1[0:64, :], bT1)
                        TT("vector", v4(tmph[0:64, :]), hmb, v4(rs[0:64, :]), ALU.mult, bHm + [bRs], [bTmph])
                        TT("vector", v4(tmph[0:64, :]), v4(tmph[0:64, :]), PR(52, 56)[0:64, :].unsqueeze(2).broadcast_to([64, 4, 128]), ALU.mult, [bTmph, bC], [bTmph])
                        TT("vector", v4(pieceb[0:64, :]), v4(tmph[0:64, :]), og[0:64, :, blk], ALU.mult, [bTmph, bOg], [bPiece])
                        CP("vector", mix[0:64, 0:2, blk], v4(pieceb[0:64, :])[:, 0::2, :], [bPiece], [bMix])
                        CP("vector", mix[64:128, 0:2, blk], v4(pieceb[0:64, :])[:, 1::2, :], [bPiece], [bMix])

                    def swa_fin(bi, g, ot, obuf):
                        CP("vector", dn[0:64, 0, :], ot[64:128, :], [obuf], [bDn])
                        TT("vector", dn[0:64, 1, :].rearrange("p (h q) -> p h q", h=4), dn[0:64, 0, :].rearrange("p (h q) -> p h q", h=4),
                           esk[0:64, l, 4 * g:4 * g + 4].unsqueeze(2).broadcast_to([64, 4, 128]), ALU.add, [bDn, bC], [bDn])
                        ACT(dn[0:64, 0, :], dn[0:64, 1, :], AF.Ln, [bDn], [bDn])
                        ACT(rd[0:64, :], dn[0:64, 0, :], AF.Exp, [bDn], [bRd], scale=-1.0)
                        TT("vector", pieceb[0:64, :], ot[0:64, :], rd[0:64, :], ALU.mult, [obuf, bRd], [bPiece])
                        pv = pieceb[0:64, :].rearrange("p (c e q) -> p c e q", c=2, e=2)
                        CP("vector", mix[0:64, 2 + 2 * g:4 + 2 * g, bi * 128:(bi + 1) * 128], pv[:, :, 0, :], [bPiece], [bMix])
                        CP("vector", mix[64:128, 2 + 2 * g:4 + 2 * g, bi * 128:(bi + 1) * 128], pv[:, :, 1, :], [bPiece], [bMix])

                    def swa_s2(bi, g, cur, k3, first, last, ot, obuf):
                        MM(ot[:, :], Vs[:, bi + cur, g, :], Pt[:, k3, :], first, last, [bVs, bPt[k3]], [obuf])
                        if last:
                            swa_fin(bi, g, ot, obuf)

                    def swa_block(bi):
                        pend = []
                        n = 4 * i + bi
                        for g in range(2):
                            ot, obuf = o_bank()
                            kbs = ([0] if n > 0 else []) + [1]
                            for kk, cur in enumerate(kbs):
                                st, sbuf_ = sc_bank()
                                kcol = (bi + cur) * 128
                                MM(st[:, :], KTs[:, g, kcol:kcol + 128], QTs[:, :, bi * 128:(bi + 1) * 128], True, False, [bKTs, bQTs], [sbuf_])
                                MM(st[:, :], IDB, nmS[:, cur, :], False, True, [bNmS, bC], [sbuf_])
                                k3 = ptc[0] % NPT
                                ptc[0] += 1
                                ACT(Pt[:, k3, :], st[:, :], AF.Exp, [sbuf_], [bPt[k3]], scale=0.125)
                                pend.append((bi, g, cur, k3, kk == 0, kk == len(kbs) - 1, ot, obuf))
                                if len(pend) > 1:
                                    swa_s2(*pend.pop(0))
                        while pend:
                            swa_s2(*pend.pop(0))

                    ml_s1(0)
                    for bi in range(4):
                        if bi + 1 < 4:
                            ml_s1(bi + 1)
                        swa_block(bi)
                        ml_s2(bi)

                    CP("gpsimd", KTs[:, :, 0:128], KTs[:, :, 512:640], [bKTs], [bKTs])
                    CP("gpsimd", Vs[:, 0, :, 0:64], Vs[:, 4, :, 0:64], [bVs], [bVs])

                    if MBSTOP < 8:
                        continue
                    if dbg and l == 0:
                        DMA(dmix[:, :, t0:t0 + 512], mix, [bMix, bMixD], [], key="dbgmix")
                    for oc in range(8):
                        pt, pbuf = gen_bank()
                        for kc in range(8):
                            MM(pt[:, :], wout[:, kc, oc * 128:(oc + 1) * 128], mix[:, kc, :], kc == 0, kc == 7, [bWout, bMix, bMixD], [pbuf])
                        CP("vector", yv[:, oc, :], pt[:, :], [pbuf], [bY])
                    ACT(sq.rearrange("p a b -> p (a b)"), yv.rearrange("p a b -> p (a b)"), AF.Square, [bY], [bSq])
                    for kc in range(8):
                        MM(pst[6][:, :], ONESB, sq[:, kc, :], kc == 0, kc == 7, [bSq, bC], [pb[6]])
                    norm_rstd(pst[6][:, :], pb[6], 128, 1.0 / D, rs, bRs, tm1, bT1)
                    for oc in range(8):
                        STT(yv[:, oc, :], yv[:, oc, :], PR(8 + oc, 9 + oc), rs, ALU.mult, ALU.mult, [bY, bRs, bC], [bY])
                    TT("vector", yv.rearrange("p a b -> p (a b)"), yv.rearrange("p a b -> p (a b)"), xt.rearrange("p a b -> p (a b)"), ALU.add, [bY, bX], [bY])
                    DMA(sA.rearrange("(k p) t -> p k t", p=128)[:, :, t0:t0 + 512], yv, [bY], [dbuf("sA", 2 * i), dbuf("sA", 2 * i + 1)], key="xst")

            if phases is None or "ffn" in phases:
                fsrc, fsrcn = (sA, "sA") if (phases is None or "mix" in phases) else (src, srcn)
                A.reset()
                wup = A.tile([8, DFF], BF16)
                wdn = A.tile([32, D], BF16)
                bWup, bWdn = A.buf("wup"), A.buf("wdn")
                stg2 = A.tile([2, 2048], F32)
                stage = [stg2[:, 0, :], stg2[:, 1, :]]
                bSt = [A.buf("st0"), A.buf("st1")]
                xt2 = A.tile([2, 8, 256], F32)
                bX2 = [A.buf("x0"), A.buf("x1")]
                hT2 = A.tile([2, 8, 256], BF16)
                bH2 = [A.buf("hT0"), A.buf("hT1")]
                sqp = A.tile([8, 256], BF16)
                bSqp = A.buf("sqp")
                sqo = A.tile([8, 256], BF16)
                bSqo = A.buf("sqo")
                rs2 = A.tile([2, 256], F32)
                bRs2 = [A.buf("rs0"), A.buf("rs1")]
                rso = A.tile([256], F32)
                bRso = A.buf("rso")
                tm1 = A.tile([256], F32)
                bT1 = A.buf("tm1")
                tm2 = A.tile([256], F32)
                bT2 = A.buf("tm2")
                rl = A.tile([2, 256], F32)
                bRl = [A.buf("rl0"), A.buf("rl1")]
                u2 = stg2.rearrange("p a b -> p (a b)").bitcast(BF16).rearrange("p (a b) -> p a b", a=32)
                bU2 = A.buf("u2")
                yv = A.tile([8, 256], F32)
                bY = A.buf("y")
                cnt = [0]
                for kc in range(8):
                    load_weight(wup[:, kc, :], w_up[l, kc * 128:(kc + 1) * 128, :], DFF, PR(16 + kc, 17 + kc), bWup, stage, bSt, cnt)
                for fc in range(32):
                    load_weight(wdn[:, fc, :], w_down[l, fc * 128:(fc + 1) * 128, :], D, None, bWdn, stage, bSt, cnt)
                gen = [0]

                def ffn_pro(i):
                    t0 = i * 256
                    xs = i % 2
                    xt = xt2[:, xs]
                    DMA(xt, fsrc.rearrange("(k p) t -> p k t", p=128)[:, :, t0:t0 + 256], [dbuf(fsrcn, i)], [bX2[xs]], key="xld%d" % xs)
                    ACT(sqp.rearrange("p a b -> p (a b)"), xt.rearrange("p a b -> p (a b)"), AF.Square, [bX2[xs]], [bSqp])
                    for kc in range(8):
                        MM(pst[6][:, 0:256], ONESB, sqp[:, kc, :], kc == 0, kc == 7, [bSqp, bC], [pb[6]])
                    norm_rstd(pst[6][:, 0:256], pb[6], 128, 1.0 / D, rs2[:, xs, :], bRs2[xs], tm1, bT1)
                    TT("vector", hT2[:, xs, 0:6, :], xt[:, 0:6, :], rs2[:, xs, :].unsqueeze(1).broadcast_to([128, 6, 256]), ALU.mult, [bX2[xs], bRs2[xs]], [bH2[xs]])
                    TT("gpsimd", hT2[:, xs, 6:8, :], xt[:, 6:8, :], rs2[:, xs, :].unsqueeze(1).broadcast_to([128, 2, 256]), ALU.mult, [bX2[xs], bRs2[xs]], [bH2[xs]])

                def ffn_post(i):
                    nonlocal_keys = final_keys
                    t0 = i * 256
                    xs = i % 2
                    xt = xt2[:, xs]
                    for kc in range(8):
                        MM(pst[7][:, 0:256], ONESB, sqo[:, kc, :], kc == 0, kc == 7, [bSqo, bC], [pb[7]])
                    norm_rstd(pst[7][:, 0:256], pb[7], 128, 1.0 / D, rso, bRso, tm2, bT2)
                    for oc in range(8):
                        STT(yv[:, oc, :], yv[:, oc, :], PR(24 + oc, 25 + oc), rso, ALU.mult, ALU.mult, [bY, bRso, bC], [bY])
                    TT("vector", yv.rearrange("p a b -> p (a b)"), yv.rearrange("p a b -> p (a b)"), xt.rearrange("p a b -> p (a b)"), ALU.add, [bY, bX2[xs]], [bY])
                    key = DMA(dst_f.rearrange("(k p) t -> p k t", p=128)[:, :, t0:t0 + 256], yv, [bY], [dbuf(dstn_f, i)], key="xst_%s" % dstn_f)
                    if key not in nonlocal_keys:
                        nonlocal_keys.append(key)

                ffn_pro(0)
                for i in range(NF):
                    xs = i % 2
                    for fc in range(32):
                        if fc == 6 and i > 0:
                            ffn_post(i - 1)
                        k = gen[0] % 4
                        gen[0] += 1
                        pt = pst[k][:, 0:256]
                        pbuf = pb[k]
                        for kc in range(8):
                            MM(pt, wup[:, kc, fc * 128:(fc + 1) * 128], hT2[:, xs, kc, :], kc == 0, kc == 7, [bWup, bH2[xs]], [pbuf])
                        r2 = fc % 2
                        if fc % 4 == 3:
                            TS("vector", rl[:, r2, :], pt, 0.0, None, ALU.max, None, [pbuf], [bRl[r2]])
                        else:
                            ACT(rl[:, r2, :], pt, AF.Relu, [pbuf], [bRl[r2]])
                        TT("gpsimd", u2[:, fc, :], rl[:, r2, :], rl[:, r2, :], ALU.mult, [bRl[r2]], [bU2] + (bSt if i == 0 else []))
                    if i + 1 < NF:
                        ffn_pro(i + 1)
                    for oc in range(8):
                        k = 4 + oc % 2
                        pt, pbuf = pst[k][:, 0:256], pb[k]
                        for fc in range(32):
                            MM(pt, wdn[:, fc, oc * 128:(oc + 1) * 128], u2[:, fc, :], fc == 0, fc == 31, [bWdn, bU2], [pbuf])
                        CP("vector", yv[:, oc, :], pt, [pbuf], [bY])
                    ACT(sqo.rearrange("p a b -> p (a b)"), yv.rearrange("p a b -> p (a b)"), AF.Square, [bY], [bSqo])
                ffn_post(NF - 1)
        P.emit(final_waits=[("dma", k) for k in final_keys])
    return nc, P, A


_CACHE = {}


def kernel(**inputs):
    x = np.asarray(inputs['x'], np.float32)
    B, S, _ = x.shape
    L = inputs['w_in'].shape[0]
    perm = _perm_in()
    cstv, tabs = _consts(S)
    w_in = np.ascontiguousarray(np.asarray(inputs['w_in'], np.float32)[:, :, perm])
    shared = {
        "w_in": w_in,
        "w_out": np.ascontiguousarray(np.asarray(inputs['w_out'], np.float32)),
        "w_up": np.ascontiguousarray(np.asarray(inputs['w_up'], np.float32)),
        "w_down": np.ascontiguousarray(np.asarray(inputs['w_down'], np.float32)),
        "par": np.stack([_params(inputs, l) for l in range(L)]),
        "cst": cstv, "tab": tabs,
    }
    in_maps = []
    for b in range(B):
        m = dict(shared)
        m["xT"] = np.ascontiguousarray(x[b].T)
        in_maps.append(m)
    nc, P, A = build(S, L)
    res = run_bass_kernel_spmd(nc, in_maps, core_ids=list(range(B)))
    out = np.stack([np.ascontiguousarray(r["yT"].T) for r in res.results], axis=0)
    return out.astype(np.float32)
```

```python
import math
import numpy as np
import concourse.bass as bass
import concourse.mybir as mybir
from concourse.bass_utils import run_bass_kernel_spmd
from contextlib import ExitStack

F32 = mybir.dt.float32
BF16 = mybir.dt.bfloat16
AF = mybir.ActivationFunctionType
ALU = mybir.AluOpType

D = 1024
SEQ = 8192
NB = 8
DEPTH = 2
DFF = 4096
INW = 2568
EPS = 1e-6
class Buf:
    __slots__ = ("name", "w", "r")

    def __init__(self, name):
        self.name = name
        self.w = None
        self.r = []


class Ins:
    __slots__ = ("eng", "fn", "deps", "sig", "cnt", "dma", "sem", "semval", "idx", "raw", "pos")

    def __init__(self, eng, fn, dma=False):
        self.eng = eng
        self.fn = fn
        self.deps = []
        self.sig = False
        self.cnt = 0
        self.dma = dma
        self.sem = None
        self.semval = 0
        self.idx = 0
        self.raw = ()
        self.pos = 0


class Prog:
    ENGS = ("tensor", "vector", "scalar", "gpsimd", "sync")

    def __init__(self, nc, paranoid=False):
        self.nc = nc
        self.paranoid = paranoid
        self.q = {e: [] for e in self.ENGS}
        self.dma_keys = {}
        self.all = []

    SAME_ENG_WINDOW = 4

    def _same_eng_sync(self, ins, d):
        if d.eng != ins.eng or d.dma or ins.eng == "tensor" or ins.eng == "sync":
            return False
        return (d in ins.raw) and (ins.pos - d.pos) <= self.SAME_ENG_WINDOW

    def _track(self, ins, reads, writes):
        deps = set()
        for b in reads:
            if b.w is not None:
                deps.add(b.w)
        ins.raw = frozenset(deps)
        for b in writes:
            if b.w is not None:
                deps.add(b.w)
            for r in b.r:
                deps.add(r)
        deps.discard(ins)
        ins.deps = list(deps)
        for b in reads:
            b.r.append(ins)
        for b in writes:
            b.w = ins
            b.r = []

    def op(self, eng, fn, reads=(), writes=()):
        ins = Ins(eng, fn)
        ins.idx = len(self.all)
        ins.pos = len(self.q[eng])
        self.all.append(ins)
        self._track(ins, reads, writes)
        self.q[eng].append(ins)
        return ins

    def dma(self, eng, fn, reads=(), writes=(), key=None):
        ins = Ins(eng, fn, dma=True)
        ins.idx = len(self.all)
        ins.pos = len(self.q[eng])
        self.all.append(ins)
        self._track(ins, reads, writes)
        st = self.dma_keys.setdefault(key, [0])
        st[0] += 16
        ins.sem = key
        ins.semval = st[0]
        self.q[eng].append(ins)
        return ins

    def emit(self, final_waits=()):
        nc = self.nc
        for ins in self.all:
            for d in ins.deps:
                if d.dma:
                    continue
                if d.eng == ins.eng and not self._same_eng_sync(ins, d):
                    continue
                d.sig = True
        for e in self.ENGS:
            c = 0
            for ins in self.q[e]:
                if ins.dma:
                    continue
                if ins.sig:
                    c += 1
                ins.cnt = c
        with ExitStack() as es:
            esem = {e: es.enter_context(nc.semaphore("es_" + e)) for e in self.ENGS}
            dsem = {k: es.enter_context(nc.semaphore("ds_%d" % i)) for i, k in enumerate(self.dma_keys)}
            block = es.enter_context(nc.Block())
            stats = {}

            def run(e, h):
                waited = {}
                nw = 0
                for ins in self.q[e]:
                    need = {}
                    for d in ins.deps:
                        if d.dma:
                            s, v = dsem[d.sem], d.semval
                        else:
                            if d.eng == e and not self._same_eng_sync(ins, d):
                                continue
                            s, v = esem[d.eng], d.cnt
                        if waited.get(id(s), 0) >= v:
                            continue
                        if need.get(id(s), (None, 0))[1] < v:
                            need[id(s)] = (s, v)
                    for sid, (s, v) in need.items():
                        h.wait_ge(s, v)
                        waited[sid] = v
                        nw += 1
                    r = ins.fn(h)
                    if ins.dma:
                        r.then_inc(dsem[ins.sem], 16)
                    elif ins.sig:
                        r.then_inc(esem[e], 1)
                for (kind, key_or_ins) in (final_waits if e == "sync" else ()):
                    if kind == "dma":
                        h.wait_ge(dsem[key_or_ins], self.dma_keys[key_or_ins][0])
                stats[e] = (len(self.q[e]), nw)

            @block.tensor
            def _(h):
                run("tensor", h)

            @block.vector
            def _(h):
                run("vector", h)

            @block.scalar
            def _(h):
                run("scalar", h)

            @block.gpsimd
            def _(h):
                run("gpsimd", h)

            @block.sync
            def _(h):
                run("sync", h)
        self.stats = stats
        return stats

NPAR = 208
C_ONES, C_IDENT, C_TRIU, C_NEG, C_P64, C_P32, C_MSC, C_MSP = [i * 128 for i in range(8)]
C_MD = 1024
C_RM = 1024 + 2048
C_HM = C_RM + 4
NCST = C_HM + 4

O_MQ, O_MK, O_MV, O_MO, O_MI, O_MF, O_SQ, O_SK, O_SV, O_DQ, O_DK, O_DV = 0, 256, 512, 768, 1024, 1028, 1032, 1544, 1672, 1800, 2056, 2312
FM_SQ, FM_SK, FM_MO, TMB0, NWB = 512, 1024, 1152, 1408, 1800
A_DQ, A_DK, A_DV, NWA = 0, 256, 512, 768


def _perm_in():
    cols = []
    cols += list(range(O_MQ, O_MQ + 256)) + list(range(O_MK, O_MK + 256))
    for c in range(4):
        cols += list(range(O_SQ + c * 64, O_SQ + c * 64 + 64)) + list(range(O_SQ + (4 + c) * 64, O_SQ + (4 + c) * 64 + 64))
    cols += list(range(O_SK, O_SK + 128))
    cols += list(range(O_MO, O_MO + 256))
    cols += list(range(O_MV, O_MV + 256))
    cols += list(range(O_SV, O_SV + 128)) + list(range(O_MI, O_MI + 4)) + list(range(O_MF, O_MF + 4))
    cols += list(range(O_DQ, O_DQ + 256)) + list(range(O_DK, O_DK + 256)) + list(range(O_DV, O_DV + 256))
    assert len(cols) == INW and len(set(cols)) == INW
    return np.array(cols)


def _consts(S):
    c = np.zeros((128, NCST), np.float32)
    idx = np.arange(128)
    c[:, C_ONES:C_ONES + 128] = 1.0
    c[:, C_IDENT:C_IDENT + 128] = np.eye(128, dtype=np.float32)
    c[:, C_TRIU:C_TRIU + 128] = (idx[:, None] <= idx[None, :])
    c[:, C_NEG:C_NEG + 128] = np.where(idx[:, None] <= idx[None, :], 0.0, -30000.0)
    for (off, hd) in ((C_P64, 64), (C_P32, 32)):
        half = hd // 2
        d = idx % hd
        partner = np.where(d < half, idx + half, idx - half)
        pm = np.zeros((128, 128), np.float32)
        pm[partner, idx] = 1.0
        c[:, off:off + 128] = pm
    c[:, C_MSC:C_MSC + 128] = (idx[:, None] <= idx[None, :])
    c[:, C_MSP:C_MSP + 128] = (idx[:, None] > idx[None, :])
    q = np.arange(512)
    for o in range(4):
        c[:, C_MD + o * 512:C_MD + (o + 1) * 512] = ((o * 128 + idx)[:, None] <= q[None, :])
    for j in range(4):
        c[32 * j:32 * j + 32, C_RM + j] = 1.0
    c[0:64, C_HM] = 1.0
    c[64:128, C_HM + 1] = 1.0
    tabs = np.zeros((4, 128, S), np.float32)
    pos = np.arange(S).astype(np.float32)
    for ti, hd in ((0, 64), (2, 32)):
        half = hd // 2
        inv = (np.float32(10000.0) ** (-np.arange(half, dtype=np.float32) * np.float32(2.0) / np.float32(hd))).astype(np.float32)
        ang = (pos[:, None] * inv[None, :]).astype(np.float32)
        d = idx % hd
        fi = d % half
        sign = np.where(d < half, -1.0, 1.0).astype(np.float32)
        tabs[ti] = np.cos(ang)[:, fi].T
        tabs[ti + 1] = (np.sin(ang)[:, fi] * sign[None, :]).T
    return c, tabs


def _params(inp, l):
    p = np.zeros((128, NPAR), np.float32)
    fm = lambda v: np.asarray(v, np.float32).reshape(-1, 128).T
    p[:, 0:8] = fm(inp['g_pre_mix'][l])
    p[:, 8:16] = fm(inp['g_post_mix'][l])
    p[:, 16:24] = fm(inp['g_pre_mlp'][l])
    p[:, 24:32] = fm(inp['g_post_mlp'][l])
    cw = np.asarray(inp['conv_w'][l], np.float32)
    p[:, 32:48] = cw.reshape(4, 4, 128).transpose(2, 1, 0).reshape(128, 16)
    p[:, 48:52] = fm(inp['conv_b'][l])
    p[0:64, 52:56] = np.asarray(inp['m_norm_g'][l], np.float32).reshape(4, 64).T
    p[0:64, 56] = np.asarray(inp['sub_g'][l], np.float32)
    p[:, 57:61] = np.asarray(inp['i_bias'][l], np.float32)[None, :]
    p[:, 61:65] = np.asarray(inp['f_bias'][l], np.float32)[None, :]
    p[:, 65:73] = np.asarray(inp['sinks'][l], np.float32)[None, :]
    for k, nm in enumerate(('lam_q1', 'lam_k1', 'lam_q2', 'lam_k2')):
        p[:, 73 + 32 * k:73 + 32 * (k + 1)] = np.asarray(inp[nm][l], np.float32)[None, :]
    return p


class Arena:
    def __init__(self, nc, es, words):
        self.t = es.enter_context(nc.sbuf_tensor("arena", [128, words], F32))
        self.dry = False
        self.words = words
        self.off = 0
        self.bufs = []
        self.pending = []
        self.peak = 0

    def reset(self):
        s = set(self.pending)
        for b in self.bufs:
            if b.w is not None:
                s.add(b.w)
            s.update(b.r)
        latest = {}
        keep = []
        for ins in s:
            if ins.dma:
                keep.append(ins)
            else:
                if ins.eng not in latest or latest[ins.eng].idx < ins.idx:
                    latest[ins.eng] = ins
        self.pending = keep + list(latest.values())
        self.off = 0
        self.bufs = []

    def tile(self, free, dtype, parts=128):
        n = int(np.prod(free))
        w = n if dtype == F32 else (n + 1) // 2
        w = (w + 7) // 8 * 8
        if not self.dry:
            a = self.t[:, self.off:self.off + w]
        self.off += w
        self.peak = max(self.peak, self.off)
        if self.dry:
            return None
        assert self.off <= self.words, ("SBUF arena overflow", self.off, self.words)
        if dtype != F32:
            a = a.bitcast(dtype)
        a = a[:, 0:n]
        if len(free) == 2:
            a = a.rearrange("p (a b) -> p a b", a=free[0])
        elif len(free) == 3:
            a = a.rearrange("p (a b c) -> p a b c", a=free[0], b=free[1])
        elif len(free) == 4:
            a = a.rearrange("p (a b c d) -> p a b c d", a=free[0], b=free[1], c=free[2])
        return a

    def buf(self, name):
        b = Buf(name)
        b.r = list(self.pending)
        self.bufs.append(b)
        return b


import os as _os
MBSTOP = int(_os.environ.get('MBSTOP', '99'))
MLSTOP = int(_os.environ.get('MLSTOP', '99'))
VARX = _os.environ.get('VARX', '')


def build(S=SEQ, L=DEPTH, phases=None, arena_kb=196, dbg=False, dry=False):
    NT = S // 512
    NF = S // 256
    nc = bass.Bass("TRN2", target_bir_lowering=False)

    def dt(n, s, k="ExternalInput"):
        return nc.dram_tensor(n, s, F32, kind=k).ap()
    xT = dt("xT", [D, S])
    w_in = dt("w_in", [L, D, INW])
    w_out = dt("w_out", [L, D, D])
    w_up = dt("w_up", [L, D, DFF])
    w_down = dt("w_down", [L, DFF, D])
    par = dt("par", [L, 128, NPAR])
    cst = dt("cst", [128, NCST])
    tab = dt("tab", [4, 128, S])
    yT = dt("yT", [D, S], "ExternalOutput")
    sA = dt("sA", [D, S], "ExternalOutput" if dbg else "Internal")
    sB = dt("sB", [D, S], "Internal")
    sD = nc.dram_tensor("sD", [4, 64, S], BF16, kind="Internal").ap()
    dmix = nc.dram_tensor("dmix", [128, 8, S], BF16, kind="ExternalOutput").ap() if dbg else None
    dbufs = {}

    def dbuf(name, j):
        k = (name, j)
        if k not in dbufs:
            dbufs[k] = Buf("d_%s_%d" % k)
        return dbufs[k]

    P = Prog(nc)

    def MM(out, lhsT, rhs, start, stop, r, w):
        P.op("tensor", lambda h: h.matmul(out, lhsT=lhsT, rhs=rhs, start=start, stop=stop), r, w)

    def ACT(out, in_, func, r, w, bias=None, scale=None):
        kw = {}
        if bias is not None:
            kw['bias'] = bias
        if scale is not None:
            kw['scale'] = scale
        P.op("scalar", lambda h: h.activation(out=out, in_=in_, func=func, **kw), r, w)

    def TT(eng, out, in0, in1, op, r, w):
        P.op(eng, lambda h: h.tensor_tensor(out=out, in0=in0, in1=in1, op=op), r, w)

    def TS(eng, out, in0, s1, s2, op0, op1, r, w):
        if op1 is None:
            P.op(eng, lambda h: h.tensor_scalar(out=out, in0=in0, scalar1=s1, scalar2=None, op0=op0), r, w)
        else:
            P.op(eng, lambda h: h.tensor_scalar(out=out, in0=in0, scalar1=s1, scalar2=s2, op0=op0, op1=op1), r, w)

    def STT(out, in0, sc, in1, op0, op1, r, w):
        P.op("vector", lambda h: h.scalar_tensor_tensor(out=out, in0=in0, scalar=sc, in1=in1, op0=op0, op1=op1), r, w)

    def CP(eng, out, in_, r, w):
        P.op(eng, lambda h: h.tensor_copy(out=out, in_=in_), r, w)

    def RCP(out, in_, r, w):
        P.op("vector", lambda h: h.reciprocal(out=out, in_=in_), r, w)

    def MSET(eng, ap, val, w):
        P.op(eng, lambda h: h.memset(ap, val), (), w)

    dcount = [0]

    def DMA(out, in_, r, w, key=None):
        if key is None:
            key = "k%d" % dcount[0]
            dcount[0] += 1
        P.dma("sync", lambda h: h.dma_start(out=out, in_=in_), r, w, key)
        return key

    final_keys = []
    with ExitStack() as es:
        sb = lambda n, s, d: es.enter_context(nc.sbuf_tensor(n, s, d))
        cf = sb("cf", [128, 4, 128], F32)
        cm = sb("cm", [128, 8], F32)
        cb = sb("cb", [128, 6, 128], BF16)
        cmd = sb("cmd", [128, 4, 512], BF16)
        prm = sb("prm", [128, L, NPAR], F32)
        epsT = sb("epsT", [128, 1], F32)
        lamT = sb("lamT", [128, L, 4], F32)
        sgl = sb("sgl", [128, L], F32)
        esk = sb("esk", [128, L, 8], F32)
        pst = [es.enter_context(nc.psum_tensor("ps%d" % i, [128, 512], F32)) for i in range(8)]
        pb = [Buf("ps%d" % i) for i in range(8)]
        pb7 = [Buf("ps7_%d" % i) for i in range(4)]
        A = Arena(nc, es, arena_kb * 256)
        bC = Buf("consts")

        stg = A.tile([3080], F32)
        bS = A.buf("cstg")
        DMA(stg, cst[:, 0:3080], [], [bS])
        DMA(prm[:], par.rearrange("l p n -> p l n"), [], [bC])
        CP("vector", cf[:, 0, :], stg[:, C_ONES:C_ONES + 128], [bS], [bC])
        CP("vector", cf[:, 1, :], stg[:, C_IDENT:C_IDENT + 128], [bS], [bC])
        CP("vector", cf[:, 2, :], stg[:, C_TRIU:C_TRIU + 128], [bS], [bC])
        CP("vector", cf[:, 3, :], stg[:, C_NEG:C_NEG + 128], [bS], [bC])
        CP("vector", cm[:, 0:6], stg[:, C_RM:C_RM + 6], [bS], [bC])
        for k, off in enumerate((C_ONES, C_IDENT, C_P64, C_P32, C_MSC, C_MSP)):
            CP("vector", cb[:, k, :], stg[:, off:off + 128], [bS], [bC])
        TS("vector", cmd[:].rearrange("p a b -> p (a b)"), stg[:, C_MD:C_MD + 2048], 30000.0, -30000.0, ALU.mult, ALU.add, [bS], [bC])
        MSET("vector", epsT[:], EPS, [bC])
        ONESF, IDF, TRIU, NEGM = cf[:, 0, :], cf[:, 1, :], cf[:, 2, :], cf[:, 3, :]
        ONESB, IDB, P64B, P32B, MSC, MSP = [cb[:, k, :] for k in range(6)]
        for l in range(L):
            lam_init = 0.8 - 0.6 * math.exp(-0.3 * l)
            tmp = A.tile([64], F32)
            bt = A.buf("lamtmp")
            for k in range(2):
                TT("vector", tmp[:, 32 * k:32 * k + 32], prm[:, l, 73 + 64 * k:73 + 64 * k + 32], prm[:, l, 105 + 64 * k:105 + 64 * k + 32], ALU.mult, [bC], [bt])
                P.op("vector", lambda h, o=lamT[:, l, 2 + k:3 + k], i=tmp[:, 32 * k:32 * k + 32]: h.reduce_sum(out=o, in_=i, axis=mybir.AxisListType.X), [bt], [bC])
            ACT(lamT[:, l, 2:4], lamT[:, l, 2:4], AF.Exp, [bC], [bC])
            TT("vector", lamT[:, l, 0:1], lamT[:, l, 2:3], lamT[:, l, 3:4], ALU.subtract, [bC], [bC])
            TS("vector", lamT[:, l, 0:1], lamT[:, l, 0:1], lam_init, None, ALU.add, None, [bC], [bC])
            TS("vector", lamT[:, l, 1:2], lamT[:, l, 0:1], -1.0, None, ALU.mult, None, [bC], [bC])
            TS("vector", sgl[:, l:l + 1], prm[:, l, 56:57], 1.0 - lam_init, None, ALU.mult, None, [bC], [bC])
            ACT(esk[:, l, :], prm[:, l, 65:73], AF.Exp, [bC], [bC])

        def load_weight(dst, src_rows, ncols, scale_ap, bW, stage, bSt, cnt):
            for c0 in range(0, ncols, 2048):
                c1 = min(ncols, c0 + 2048)
                s = cnt[0] % 2
                cnt[0] += 1
                DMA(stage[s][:, 0:c1 - c0], src_rows[:, c0:c1], [], [bSt[s]], key="wst%d" % s)
                if s == 1:
                    ACT(dst[:, c0:c1], stage[s][:, 0:c1 - c0], AF.Copy, [bSt[s], bC], [bW], scale=scale_ap)
                elif scale_ap is None:
                    CP("vector", dst[:, c0:c1], stage[s][:, 0:c1 - c0], [bSt[s]], [bW])
                else:
                    TS("vector", dst[:, c0:c1], stage[s][:, 0:c1 - c0], scale_ap, None, ALU.mult, None, [bSt[s], bC], [bW])

        def norm_rstd(ss_ps, bps, nparts, inv_n, rs_out, brs, tmp, btmp):
            ACT(tmp, ss_ps, AF.Ln, [bps, bC], [btmp], bias=epsT[0:nparts, :], scale=inv_n)
            ACT(rs_out, tmp, AF.Exp, [btmp], [brs], scale=-0.5)

        for l in range(L):
            src = xT if l == 0 else sB
            srcn = "xT" if l == 0 else "sB"
            dst_f = yT if l == L - 1 else sB
            dstn_f = "yT" if l == L - 1 else "sB"
            PR = lambda a, b, _l=l: prm[:, _l, a:b]
            if phases is None or "mix" in phases or "mixA" in phases or "mixB" in phases:
                A.reset()
                win = A.tile([8, NWA], BF16)
                bWin = A.buf("winA")
                KTd = A.tile([2, S], BF16)
                bKTdT = [A.buf("KTd%d" % k) for k in range(NT)]
                Vd = A.tile([S // 128, 4, 128], BF16)
                bVdT = [A.buf("Vd%d" % k) for k in range(NT)]
                xt = A.tile([8, 512], F32)
                bX = A.buf("x")
                qdp_raw = A.tile([2, 2048], F32)
                Qdp2 = [qdp_raw[:, k, :].bitcast(BF16).rearrange("p (a b) -> p a b", a=8) for k in range(2)]
                bQdp2 = [A.buf("Qdp0"), A.buf("Qdp1")]
                stage = [xt.rearrange("p a b -> p (a b)")[:, 0:2048], qdp_raw[:, 0, :]]
                bSt = [bX, bQdp2[0]]
                hT = A.tile([8, 512], BF16)
                bH = A.buf("hT")
                sq = A.tile([8, 512], BF16)
                bSq = A.buf("sq")
                rs = A.tile([512], F32)
                bRs = A.buf("rs")
                tm1 = A.tile([512], F32)
                bT1 = A.buf("tm1")
                tbl = A.tile([2, 512], F32)
                bTb = A.buf("tbl")
                qb = A.tile([512], BF16)
                bQb = A.buf("qb")
                t1 = A.tile([512], F32)
                t2 = A.tile([512], F32)
                bT1r, bT2r = A.buf("t1"), A.buf("t2")
                qr = A.tile([512], BF16)
                bQr = A.buf("qr")
                rd = A.tile([2, 512], F32)
                bRd = [A.buf("rd0"), A.buf("rd1")]
                sqh = A.tile([512], BF16)
                bSqh = A.buf("sqh")
                dpc = A.tile([4, 512], BF16)
                bDpc = A.buf("dpc")
                NPT = 6
                SCB = [2, 3, 7]
                Pt = A.tile([NPT, 512], BF16)
                bPt = [A.buf("Pt%d" % k) for k in range(NPT)]
                Aj = A.tile([2, 512], F32)
                bAj = [A.buf("Aj0"), A.buf("Aj1")]
                Dh = A.tile([512], F32)
                bDh = A.buf("Dh")
                cnt = [0]
                for kc in range(8):
                    load_weight(win[:, kc, :], w_in[l, kc * 128:(kc + 1) * 128, NWB:NWB + NWA], NWA, PR(kc, kc + 1), bWin, stage, bSt, cnt)
                MSET("gpsimd", Vd[:, :, :, 64:128], 1.0, bVdT)
                gen = [0]
                scn = [0]
                oac = [0]
                ptc = [0]

                def gen_bank():
                    k = gen[0] % 2
                    gen[0] += 1
                    return pst[k], pb[k]

                def sc_bank():
                    k = 2 + scn[0] % 2
                    scn[0] += 1
                    return pst[k], pb[k]

                def o_bank():
                    k = 4 + oac[0] % 2
                    oac[0] += 1
                    return pst[k], pb[k]

                def prologue(i, xt, bX, tbl, bTb, tsl, hT, bH, sq, bSq, rs, bRs, tm1, bT1):
                    t0 = i * 512
                    DMA(xt, src.rearrange("(k p) t -> p k t", p=128)[:, :, t0:t0 + 512],
                        [dbuf(srcn, 2 * i), dbuf(srcn, 2 * i + 1)], [bX], key="xld")
                    DMA(tbl, tab.rearrange("f p t -> p f t")[:, tsl, t0:t0 + 512], [], [bTb], key="tbl")
                    ACT(sq.rearrange("p a b -> p (a b)"), xt.rearrange("p a b -> p (a b)"), AF.Square, [bX], [bSq])
                    for kc in range(8):
                        MM(pst[6][:, :], ONESB, sq[:, kc, :], kc == 0, kc == 7, [bSq, bC], [pb[6]])
                    norm_rstd(pst[6][:, :], pb[6], 128, 1.0 / D, rs, bRs, tm1, bT1)
                    TT("vector", hT[:, 0:6, :], xt[:, 0:6, :], rs.unsqueeze(1).broadcast_to([128, 6, 512]), ALU.mult, [bX, bRs], [bH])
                    TT("gpsimd", hT[:, 6:8, :], xt[:, 6:8, :], rs.unsqueeze(1).broadcast_to([128, 2, 512]), ALU.mult, [bX, bRs], [bH])

                def inproj_fm(col0, m=128):
                    pt, pbuf = gen_bank()
                    for kc in range(8):
                        MM(pt[0:m, :], win[:, kc, col0:col0 + m], hT[:, kc, :], kc == 0, kc == 7, [bWin, bH], [pbuf])
                    return pt, pbuf

                def rope_chunk(col0, permb, dest, bdest):
                    pt, pbuf = inproj_fm(col0)
                    CP("vector", qb, pt[:, :], [pbuf], [bQb])
                    MM(pst[6][:, :], permb, qb, True, True, [bQb, bC], [pb[6]])
                    TT("vector", t1, pt[:, :], tbl[:, 0, :], ALU.mult, [pbuf, bTb], [bT1r])
                    TT("vector", t2, pst[6][:, :], tbl[:, 1, :], ALU.mult, [pb[6], bTb], [bT2r])
                    TT("vector", dest, t1, t2, ALU.add, [bT1r, bT2r], [bdest])

                def f_units(i):
                    t0 = i * 512
                    qs = i % 2
                    us = []
                    us.append(lambda: prologue(i, xt, bX, tbl, bTb, slice(2, 4), hT, bH, sq, bSq, rs, bRs, tm1, bT1))

                    def uq(c):
                        rope_chunk(A_DQ + c * 128, P32B, qr, bQr)
                        for jj in range(4):
                            TS("vector", Qdp2[qs][:, 4 * c + jj, :], qr, cm[:, jj:jj + 1], None, ALU.mult, None, [bQr, bC], [bQdp2[qs]])

                    def uk(c):
                        rope_chunk(A_DK + c * 128, P32B, KTd[:, c, t0:t0 + 512], bKTdT[i])

                    def uv(bi):
                        pt, pbuf = gen_bank()
                        for kc in range(8):
                            MM(pt[:, 0:256], hT[:, kc, bi * 128:(bi + 1) * 128], win[:, kc, A_DV:A_DV + 256], kc == 0, kc == 7, [bWin, bH], [pbuf])
                        CP("vector", Vd[:, 4 * i + bi, :, 0:64], pt[:, 0:256].rearrange("p (h d) -> p h d", h=4), [pbuf], [bVdT[i]])
                    for c in range(2):
                        us.append(lambda c=c: uq(c))
                    for c in range(2):
                        us.append(lambda c=c: uk(c))
                    for bi in range(4):
                        us.append(lambda bi=bi: uv(bi))
                    return us

                NTA = NT if (phases is None or 'mix' in phases or 'mixA' in phases) else 0
                if NTA:
                    for u in f_units(0):
                        u()
                for i in range(NTA):
                    t0 = i * 512
                    Qdp = Qdp2[i % 2]
                    bQdp = bQdp2[i % 2]
                    nxt = f_units(i + 1) if i + 1 < NTA else []
                    nxt_all = list(nxt)
                    nkb = 4 * (i + 1)
                    LAG = 2
                    n_items = 8 * nkb
                    item = [0]
                    pend = []
                    deferred = []

                    def tick():
                        for d_ in deferred:
                            d_[0] -= 1
                        while deferred and deferred[0][0] <= 0:
                            deferred.pop(0)[1]()

                    def fin_a(j, ot, obuf):
                        h4 = j // 2
                        jp = j % 2
                        RCP(rd[0:64, jp, :], ot[64:128, :], [obuf], [bRd[jp]])
                        TT("vector", Aj[0:64, jp, :], ot[0:64, :], rd[0:64, jp, :], ALU.mult, [obuf, bRd[jp]], [bAj[jp]])
                        if jp == 1:
                            STT(Dh[0:64, :], Aj[0:64, 1, :], lamT[0:64, l, 1:2], Aj[0:64, 0, :], ALU.mult, ALU.add, [bAj[0], bAj[1], bC], [bDh])
                            TT("vector", sqh[0:64, :], Dh[0:64, :], Dh[0:64, :], ALU.mult, [bDh], [bSqh])

                            def part_b(h4=h4):
                                MM(pst[6][0:64, :], ONESB[0:64, 0:64], sqh[0:64, :], True, True, [bSqh, bC], [pb[6]])
                                norm_rstd(pst[6][0:64, :], pb[6], 64, 1.0 / 64, rs[0:64, :], bRs, tm1[0:64, :], bT1)
                                STT(dpc[0:64, h4, :], Dh[0:64, :], sgl[0:64, l:l + 1], rs[0:64, :], ALU.mult, ALU.mult, [bDh, bRs, bC], [bDpc])
                            deferred.append([5, part_b])

                    def stage2(j, kb, k3, ot, obuf):
                        MM(ot[:, :], Vd[:, kb, j // 2, :], Pt[:, k3, :], kb == 0, kb == nkb - 1, [bVdT[kb // 4], bPt[k3]], [obuf])
                        if kb == nkb - 1:
                            fin_a(j, ot, obuf)

                    for j in range(8):
                        c = j // 4
                        ot, obuf = o_bank()
                        for kb in range(nkb):
                            sk_ = SCB[scn[0] % len(SCB)]
                            scn[0] += 1
                            st, sbuf_ = pst[sk_], pb[sk_]
                            diag = kb >= 4 * i
                            MM(st[:, :], KTd[:, c, kb * 128:(kb + 1) * 128], Qdp[:, j, :], True, not diag, [bKTdT[kb // 4], bQdp], [sbuf_])
                            if diag:
                                MM(st[:, :], IDB, cmd[:, kb - 4 * i, :], False, True, [bC], [sbuf_])
                            k3 = ptc[0] % NPT
                            ptc[0] += 1
                            ACT(Pt[:, k3, :], st[:, :], AF.Exp, [sbuf_], [bPt[k3]], scale=32.0 ** -0.5)
                            pend.append((j, kb, k3, ot, obuf))
                            if len(pend) > LAG:
                                stage2(*pend.pop(0))
                            tick()
                            item[0] += 1
                            if nxt and item[0] * (len(nxt_all) + 1) >= (len(nxt_all) - len(nxt) + 1) * n_items:
                                nxt.pop(0)()
                    while pend:
                        stage2(*pend.pop(0))
                    while deferred:
                        deferred.pop(0)[1]()
                    while nxt:
                        nxt.pop(0)()
                    DMA(sD[:, :, t0:t0 + 512].rearrange("h p t -> p h t"), dpc[0:64, :, :], [bDpc], [dbuf("sD", i)], key="dst")

            if phases is None or "mix" in phases or "mixB" in phases:
                A.reset()
                win = A.tile([8, NWB], BF16)
                wout = A.tile([8, D], BF16)
                bWin, bWout = A.buf("win"), A.buf("wout")
                xt2m = A.tile([2, 8, 512], F32)
                bX2m = [A.buf("x0"), A.buf("x1")]
                yv = A.tile([8, 512], F32)
                bY = A.buf("y")
                stage = [xt2m[:, 0].rearrange("p a b -> p (a b)")[:, 0:2048], yv.rearrange("p a b -> p (a b)")[:, 0:2048]]
                bSt = [bX2m[0], bY]
                hT = A.tile([8, 512], BF16)
                bH = A.buf("hT")
                sq = A.tile([8, 512], BF16)
                bSq = A.buf("sq")
                rs = A.tile([512], F32)
                bRs = A.buf("rs")
                tm1 = A.tile([512], F32)
                bT1 = A.buf("tm1")
                tbl = A.tile([2, 512], F32)
                bTb = A.buf("tbl")
                pre = A.tile([515], F32)
                halo = A.tile([4, 3], F32)
                bPre = A.buf("pre")
                bHalo = A.buf("halo")
                acc = A.tile([512], F32)
                bAcc = A.buf("acc")
                qm = A.tile([2, 512], BF16)
                km = A.tile([2, 512], BF16)
                bQm, bKm = A.buf("qm"), A.buf("km")
                qmp = A.tile([4, 512], BF16)
                bQmp = A.buf("qmp")
                og = A.tile([4, 512], BF16)
                bOg = A.buf("og")
                qb = A.tile([512], BF16)
                bQb = A.buf("qb")
                t12 = A.tile([2, 512], F32)
                t1, t2 = t12[:, 0, :], t12[:, 1, :]
                bT1r, bT2r = A.buf("t1"), A.buf("t2")
                dn = t12
                rd = acc
                tmph = pre[:, 0:512]
                sqh = qb
                qr = A.tile([512], BF16)
                bQr = A.buf("qr")
                QTs = A.tile([4, 512], BF16)
                bQTs = A.buf("QTs")
                KTs = A.tile([2, 640], BF16)
                bKTs = A.buf("KTs")
                Vs = A.tile([5, 2, 128], BF16)
                bVs = A.buf("Vs")
                Vm = A.tile([4, 4, 128], BF16)
                bVm = A.buf("Vm")
                gt = A.tile([4, 8], F32)
                bGt = A.buf("gt")
                lft = A.tile([4, 4], F32)
                bLf = A.buf("lft")
                gtmp = A.tile([4, 4], F32)
                bGtmp = A.buf("gtmp")
                u4d = A.tile([2, 4], F32)
                w4d = A.tile([2, 4], F32)
                decd = A.tile([2, 4], F32)
                LFbd = A.tile([2, 4, 128], F32)
                EBd = A.tile([2, 4, 128], F32)
                DTd = A.tile([2, 4, 128], F32)
                scd = A.tile([2, 4, 128], BF16)
                Qsd = A.tile([2, 4, 128], BF16)
                mlb = {nm: [A.buf(nm + "0"), A.buf(nm + "1")] for nm in ("u4", "w4", "dec", "LFb", "EB", "DT", "sc", "Qs")}
                Kw4 = A.tile([256], BF16)
                bKw = A.buf("Kw")
                C32 = A.tile([4, 128], F32)
                Cbf = A.tile([4, 128], BF16)
                bC32A = A.buf("C32")
                bCbfA = A.buf("Cbf")
                hmt = A.tile([2, 4, 128], F32)
                bHm = [A.buf("hm0"), A.buf("hm1")]
                pieceb = A.tile([512], BF16)
                bPiece = A.buf("piece")
                mix = A.tile([8, 512], BF16)
                bMix = A.buf("mix")
                bMixD = A.buf("mixD")
                NPT = 4
                Pt = A.tile([NPT, 512], BF16)
                bPt = [A.buf("Pt%d" % k) for k in range(NPT)]
                cnt = [0]
                for kc in range(8):
                    load_weight(win[:, kc, :], w_in[l, kc * 128:(kc + 1) * 128, 0:NWB], NWB, PR(kc, kc + 1), bWin, stage, bSt, cnt)
                for kc in range(8):
                    load_weight(wout[:, kc, :], w_out[l, kc * 128:(kc + 1) * 128, :], D, None, bWout, stage, bSt, cnt)
                nmS = A.tile([2, 512], BF16)
                bNmS = A.buf("nmS")
                for m_, msk_ in enumerate((MSP, MSC)):
                    TS("vector", nmS[:, m_, :].rearrange("p (h q) -> p h q", h=4), msk_.unsqueeze(1).broadcast_to([128, 4, 128]), 30000.0, -30000.0, ALU.mult, ALU.add, [bC], [bNmS])
                MSET("gpsimd", halo[:], 0.0, [bHalo])
                MSET("gpsimd", KTs[:], 0.0, [bKTs])
                MSET("gpsimd", Vs[:], 0.0, [bVs])
                MSET("gpsimd", Vs[:, :, :, 64:128], 1.0, [bVs])
                MSET("gpsimd", Vm[:, :, :, 64:128], 1.0, [bVm])
                MSET("vector", C32[:], 0.0, [bC32A])
                MSET("vector", Cbf[:], 0.0, [bCbfA])
                gen = [0]
                scn = [0]
                oac = [0]
                ptc = [0]

                def pro_load(i):
                    t0 = i * 512
                    xs_ = i % 2
                    DMA(xt2m[:, xs_], src.rearrange("(k p) t -> p k t", p=128)[:, :, t0:t0 + 512],
                        [dbuf(srcn, 2 * i), dbuf(srcn, 2 * i + 1)], [bX2m[xs_]], key="xldm%d" % xs_)
                    DMA(tbl, tab.rearrange("f p t -> p f t")[:, 0:2, t0:t0 + 512], [], [bTb], key="tbl")

                def pro_rest(i):
                    xs_ = i % 2
                    xt_, bX_ = xt2m[:, xs_], bX2m[xs_]
                    ACT(sq.rearrange("p a b -> p (a b)"), xt_.rearrange("p a b -> p (a b)"), AF.Square, [bX_], [bSq])
                    for kc in range(8):
                        MM(pst[6][:, :], ONESB, sq[:, kc, :], kc == 0, kc == 7, [bSq, bC], [pb[6]])
                    norm_rstd(pst[6][:, :], pb[6], 128, 1.0 / D, rs, bRs, tm1, bT1)
                    TT("vector", hT[:, 0:6, :], xt_[:, 0:6, :], rs.unsqueeze(1).broadcast_to([128, 6, 512]), ALU.mult, [bX_, bRs], [bH])
                    TT("gpsimd", hT[:, 6:8, :], xt_[:, 6:8, :], rs.unsqueeze(1).broadcast_to([128, 2, 512]), ALU.mult, [bX_, bRs], [bH])

                pro_load(0)
                pro_rest(0)
                for i in range(NT):
                    t0 = i * 512
                    xt, bX = xt2m[:, i % 2], bX2m[i % 2]
                    for h4 in range(4):
                        DMA(mix[(h4 % 2) * 64:(h4 % 2) * 64 + 64, 6 + h4 // 2, :], sD[h4, :, t0:t0 + 512], [dbuf("sD", i)], [bMixD], key="dld")
                    if MBSTOP < 1:
                        continue
                    for c in range(4):
                        pt, pbuf = inproj_fm(c * 128)
                        CP("vector", pre[:, 0:3], halo[:, c, :], [bHalo], [bPre])
                        CP("vector", pre[:, 3:515], pt[:, :], [pbuf], [bPre])
                        CP("vector", halo[:, c, :], pre[:, 512:515], [bPre], [bHalo])
                        TS("vector", acc, pre[:, 3:515], PR(32 + c * 4 + 3, 32 + c * 4 + 4), PR(48 + c, 49 + c), ALU.mult, ALU.add, [bPre, bC], [bAcc])
                        for j in (2, 1, 0):
                            STT(acc, pre[:, j:j + 512], PR(32 + c * 4 + j, 32 + c * 4 + j + 1), acc, ALU.mult, ALU.add, [bPre, bAcc, bC], [bAcc])
                        if c < 2:
                            ACT(qm[:, c, :], acc, AF.Silu, [bAcc], [bQm])
                            for hh in range(2):
                                TS("vector", qmp[:, 2 * c + hh, :], qm[:, c, :], cm[:, 4 + hh:5 + hh], None, ALU.mult, None, [bQm, bC], [bQmp])
                        else:
                            ACT(t1, acc, AF.Silu, [bAcc], [bT1r])
                            TS("vector", km[:, c - 2, :], t1, 0.125, None, ALU.mult, None, [bT1r], [bKm])
                    if MBSTOP < 2:
                        continue
                    for h4 in range(4):
                        pt, pbuf = inproj_fm(FM_MO + h4 * 64, m=64)
                        ACT(og[0:64, h4, :], pt[0:64, :], AF.Sigmoid, [pbuf], [bOg])
                    if MBSTOP < 3:
                        continue
                    for c in range(4):
                        rope_chunk(FM_SQ + c * 128, P64B, QTs[:, c, :], bQTs)
                    rope_chunk(FM_SK, P64B, qr, bQr)
                    for g in range(2):
                        TS("vector", KTs[:, g, 128:640], qr, cm[:, 4 + g:5 + g], None, ALU.mult, None, [bQr, bC], [bKTs])
                    if MBSTOP < 4:
                        continue
                    for bi in range(4):
                        pt, pbuf = gen_bank()
                        for kc in range(8):
                            MM(pt[:, 0:392], hT[:, kc, bi * 128:(bi + 1) * 128], win[:, kc, TMB0:TMB0 + 392], kc == 0, kc == 7, [bWin, bH], [pbuf])
                        CP("vector", Vm[:, bi, :, 0:64], pt[:, 0:256].rearrange("p (h d) -> p h d", h=4), [pbuf], [bVm])
                        CP("vector", Vs[:, 1 + bi, :, 0:64], pt[:, 256:384].rearrange("p (g d) -> p g d", g=2), [pbuf], [bVs])
                        TT("vector", gt[:, bi, :], pt[:, 384:392], PR(57, 65), ALU.add, [pbuf, bC], [bGt])
                    TS("vector", gtmp, gt[:, :, 4:8], -1.0, None, ALU.mult, None, [bGt], [bGtmp])
                    ACT(gtmp.rearrange("p a b -> p (a b)"), gtmp.rearrange("p a b -> p (a b)"), AF.Exp, [bGtmp], [bGtmp])
                    TS("vector", gtmp, gtmp, 1.0, None, ALU.add, None, [bGtmp], [bGtmp])
                    ACT(gtmp.rearrange("p a b -> p (a b)"), gtmp.rearrange("p a b -> p (a b)"), AF.Ln, [bGtmp], [bGtmp])
                    TS("vector", lft, gtmp, -1.0, None, ALU.mult, None, [bGtmp], [bLf])

                    def ml_s1(bi):
                        blk = slice(bi * 128, (bi + 1) * 128)
                        s_ = bi % 2
                        u4, w4, dec = u4d[:, s_, :], w4d[:, s_, :], decd[:, s_, :]
                        LFb4, EB4, DT4, sc4, Qs4 = LFbd[:, s_], EBd[:, s_], DTd[:, s_], scd[:, s_], Qsd[:, s_]
                        bU4, bW4, bDec, bLFb, bEB, bDT, bSc, bQs = [mlb[nm][s_] for nm in ("u4", "w4", "dec", "LFb", "EB", "DT", "sc", "Qs")]
                        MM(pst[6][:, 0:4], TRIU, lft[:, bi, :], True, True, [bLf, bC], [pb[6]])
                        TT("vector", u4, gt[:, bi, 0:4], pst[6][:, 0:4], ALU.subtract, [bGt, pb[6]], [bU4])
                        CP("vector", LFb4, lft[:, bi, :].unsqueeze(2).broadcast_to([128, 4, 128]), [bLf], [bLFb])
                        for h in range(4):
                            MM(pst[0][:, h * 128:(h + 1) * 128], LFb4[:, h, :], TRIU, True, True, [bLFb, bC], [pb[0]])
                        for h in range(4):
                            MM(pst[1][:, h * 128:(h + 1) * 128], LFb4[:, h, :], TRIU, True, False, [bLFb, bC], [pb[1]])
                            MM(pst[1][:, h * 128:(h + 1) * 128], IDF, NEGM, False, True, [bC], [pb[1]])
                        ACT(EB4.rearrange("p a b -> p (a b)"), pst[0][:, :], AF.Exp, [pb[0]], [bEB])
                        for h in range(4):
                            ACT(DT4[:, h, :], pst[1][:, h * 128:(h + 1) * 128], AF.Exp, [pb[1], bU4], [bDT], bias=u4[:, h:h + 1])
                        CP("vector", w4, DT4[:, :, 127], [bDT], [bW4])
                        CP("vector", dec, EB4[:, :, 127], [bEB], [bDec])
                        for h in range(4):
                            MM(pst[7][:, h * 128:(h + 1) * 128], km[:, h // 2, blk], qmp[:, h, blk], True, True, [bKm, bQmp], [pb[7]])
                        TT("vector", sc4.rearrange("p a b -> p (a b)"), pst[7][:, :], DT4.rearrange("p a b -> p (a b)"), ALU.mult, [pb[7], bDT], [bSc])
                        TT("vector", Qs4, qmp[:, :, blk], EB4, ALU.mult, [bQmp, bEB], [bQs])

                    def ml_s2(bi):
                        blk = slice(bi * 128, (bi + 1) * 128)
                        s_ = bi % 2
                        u4, w4, dec = u4d[:, s_, :], w4d[:, s_, :], decd[:, s_, :]
                        LFb4, EB4, DT4, sc4, Qs4 = LFbd[:, s_], EBd[:, s_], DTd[:, s_], scd[:, s_], Qsd[:, s_]
                        bU4, bW4, bDec, bLFb, bEB, bDT, bSc, bQs = [mlb[nm][s_] for nm in ("u4", "w4", "dec", "LFb", "EB", "DT", "sc", "Qs")]
                        for h in range(4):
                            MM(pst[6][:, h * 128:(h + 1) * 128], Vm[:, bi, h, :], sc4[:, h, :], True, False, [bVm, bSc], [pb[6]])
                            MM(pst[6][:, h * 128:(h + 1) * 128], Cbf[:, h, :], Qs4[:, h, :], False, True, [bCbfA, bQs], [pb[6]])
                        CP("vector", dn[0:64, 0, :], pst[6][64:128, :], [pb[6]], [bT1r, bT2r])
                        STT(dn[0:64, 1, :], dn[0:64, 0, :], -1.0, dn[0:64, 0, :], ALU.mult, ALU.max, [bT1r, bT2r], [bT1r, bT2r])
                        TS("vector", dn[0:64, 0, :], dn[0:64, 1, :], 1.0, None, ALU.max, None, [bT1r, bT2r], [bT1r, bT2r])
                        ACT(dn[0:64, 1, :], dn[0:64, 0, :], AF.Ln, [bT1r, bT2r], [bT1r, bT2r])
                        ACT(rd[0:64, :], dn[0:64, 1, :], AF.Exp, [bT1r, bT2r], [bAcc], scale=-1.0)
                        TT("vector", hmt[0:64, s_, :, :], pst[6][0:64, :].rearrange("p (h t) -> p h t", h=4), rd[0:64, :].rearrange("p (h t) -> p h t", h=4),
                           ALU.mult, [pb[6], bAcc], [bHm[s_]])
                        for c in range(2):
                            MM(pst[7][:, c * 128:(c + 1) * 128], km[:, c, blk], IDB, True, True, [bKm, bC], [pb[7]])
                        TT("vector", Kw4.rearrange("p (h d) -> p h d", h=4), pst[7][:, 0:256].rearrange("p (h d) -> p h d", h=4),
                           w4.unsqueeze(2).broadcast_to([128, 4, 64]), ALU.mult, [pb[7], bW4], [bKw])
                        for h in range(4):
                            MM(pst[0][:, h * 128:(h + 1) * 128], Kw4[:, (h // 2) * 128:(h // 2 + 1) * 128], Vm[:, bi, h, :], True, True, [bKw, bVm], [pb[0]])
                        dC = pst[0][:, :].rearrange("p (h t) -> p h t", h=4)
                        for par in range(2):
                            r5 = slice(par * 64, par * 64 + 64)
                            TT("vector", C32[r5, par::2, :], C32[r5, par::2, :], dec[r5, par::2].unsqueeze(2).broadcast_to([64, 2, 128]), ALU.mult, [bC32A, bDec], [bC32A])
                            TT("vector", C32[r5, par::2, :], C32[r5, par::2, :], dC[r5, par::2, :], ALU.add, [bC32A, pb[0]], [bC32A])
                            CP("vector", Cbf[r5, par::2, :], C32[r5, par::2, :], [bC32A], [bCbfA])
                        hmb = hmt[0:64, s_, :, :]
                        v4 = lambda ap: ap.rearrange("p (h t) -> p h t", h=4)
                        ACT(v4(sqh[0:64, :]), hmb, AF.Square, [bHm[s_]], [bQb])
                        MM(pst[6][0:64, :], ONESB[0:64, 0:64], sqh[0:64, :], True, True, [bQb, bC], [pb[6]])
                        norm_rstd(pst[6][0:64, :], pb[6], 64, 1.0 / 64, rs[0:64, :], bRs, tm1[0:64, :], bT1)
                        TT("vector", v4(tmph[0:64, :]), hmb, v4(rs[0:64, :]), ALU.mult, [bHm[s_], bRs], [bPre])
                        TT("vector", v4(tmph[0:64, :]), v4(tmph[0:64, :]), PR(52, 56)[0:64, :].unsqueeze(2).broadcast_to([64, 4, 128]), ALU.mult, [bPre, bC], [bPre])
                        TT("vector", v4(pieceb[0:64, :]), v4(tmph[0:64, :]), og[0:64, :, blk], ALU.mult, [bPre, bOg], [bPiece])
                        CP("vector", mix[0:64, 0:2, blk], v4(pieceb[0:64, :])[:, 0::2, :], [bPiece], [bMix])
                        CP("vector", mix[64:128, 0:2, blk], v4(pieceb[0:64, :])[:, 1::2, :], [bPiece], [bMix])

                    def swa_fin(bi, g, ot, obuf):
                        CP("vector", dn[0:64, 0, :], ot[64:128, :], [obuf], [bT1r, bT2r])
                        TT("vector", dn[0:64, 1, :].rearrange("p (h q) -> p h q", h=4), dn[0:64, 0, :].rearrange("p (h q) -> p h q", h=4),
                           esk[0:64, l, 4 * g:4 * g + 4].unsqueeze(2).broadcast_to([64, 4, 128]), ALU.add, [bT1r, bT2r, bC], [bT1r, bT2r])
                        ACT(dn[0:64, 0, :], dn[0:64, 1, :], AF.Ln, [bT1r, bT2r], [bT1r, bT2r])
                        ACT(rd[0:64, :], dn[0:64, 0, :], AF.Exp, [bT1r, bT2r], [bAcc], scale=-1.0)
                        TT("vector", pieceb[0:64, :], ot[0:64, :], rd[0:64, :], ALU.mult, [obuf, bAcc], [bPiece])
                        pv = pieceb[0:64, :].rearrange("p (c e q) -> p c e q", c=2, e=2)
                        CP("vector", mix[0:64, 2 + 2 * g:4 + 2 * g, bi * 128:(bi + 1) * 128], pv[:, :, 0, :], [bPiece], [bMix])
                        CP("vector", mix[64:128, 2 + 2 * g:4 + 2 * g, bi * 128:(bi + 1) * 128], pv[:, :, 1, :], [bPiece], [bMix])

                    def swa_s2(bi, g, cur, k3, first, last, ot, obuf):
                        MM(ot[:, :], Vs[:, bi + cur, g, :], Pt[:, k3, :], first, last, [bVs, bPt[k3]], [obuf])
                        if last:
                            swa_fin(bi, g, ot, obuf)

                    def swa_block(bi):
                        pend = []
                        n = 4 * i + bi
                        for g in range(2):
                            ot, obuf = o_bank()
                            kbs = ([0] if n > 0 else []) + [1]
                            for kk, cur in enumerate(kbs):
                                st, sbuf_ = sc_bank()
                                kcol = (bi + cur) * 128
                                MM(st[:, :], KTs[:, g, kcol:kcol + 128], QTs[:, :, bi * 128:(bi + 1) * 128], True, False, [bKTs, bQTs], [sbuf_])
                                MM(st[:, :], IDB, nmS[:, cur, :], False, True, [bNmS, bC], [sbuf_])
                                k3 = ptc[0] % NPT
                                ptc[0] += 1
                                ACT(Pt[:, k3, :], st[:, :], AF.Exp, [sbuf_], [bPt[k3]], scale=0.125)
                                pend.append((bi, g, cur, k3, kk == 0, kk == len(kbs) - 1, ot, obuf))
                                if len(pend) > 1:
                                    swa_s2(*pend.pop(0))
                        while pend:
                            swa_s2(*pend.pop(0))

                    if i + 1 < NT:
                        pro_load(i + 1)
                    ml_s1(0)
                    for bi in range(4):
                        if bi + 1 < 4:
                            ml_s1(bi + 1)
                        swa_block(bi)
                        ml_s2(bi)

                    CP("gpsimd", KTs[:, :, 0:128], KTs[:, :, 512:640], [bKTs], [bKTs])
                    CP("gpsimd", Vs[:, 0, :, 0:64], Vs[:, 4, :, 0:64], [bVs], [bVs])

                    if MBSTOP < 8:
                        continue
                    if dbg and l == 0:
                        DMA(dmix[:, :, t0:t0 + 512], mix, [bMix, bMixD], [], key="dbgmix")
                    if i + 1 < NT:
                        pro_rest(i + 1)
                    for oc in range(8):
                        pt, pbuf = gen_bank()
                        for kc in range(8):
                            MM(pt[:, :], wout[:, kc, oc * 128:(oc + 1) * 128], mix[:, kc, :], kc == 0, kc == 7, [bWout, bMix, bMixD], [pbuf])
                        CP("vector", yv[:, oc, :], pt[:, :], [pbuf], [bY])
                    ACT(sq.rearrange("p a b -> p (a b)"), yv.rearrange("p a b -> p (a b)"), AF.Square, [bY], [bSq])
                    for kc in range(8):
                        MM(pst[6][:, :], ONESB, sq[:, kc, :], kc == 0, kc == 7, [bSq, bC], [pb[6]])
                    norm_rstd(pst[6][:, :], pb[6], 128, 1.0 / D, rs, bRs, tm1, bT1)
                    for oc in range(8):
                        STT(yv[:, oc, :], yv[:, oc, :], PR(8 + oc, 9 + oc), rs, ALU.mult, ALU.mult, [bY, bRs, bC], [bY])
                    TT("vector", yv.rearrange("p a b -> p (a b)"), yv.rearrange("p a b -> p (a b)"), xt.rearrange("p a b -> p (a b)"), ALU.add, [bY, bX], [bY])
                    DMA(sA.rearrange("(k p) t -> p k t", p=128)[:, :, t0:t0 + 512], yv, [bY], [dbuf("sA", 2 * i), dbuf("sA", 2 * i + 1)], key="xst")

            if phases is None or "ffn" in phases:
                fsrc, fsrcn = (sA, "sA") if (phases is None or "mix" in phases) else (src, srcn)
                A.reset()
                wup = A.tile([8, DFF], BF16)
                wdn = A.tile([32, D], BF16)
                bWup, bWdn = A.buf("wup"), A.buf("wdn")
                stg2 = A.tile([2, 2048], F32)
                stage = [stg2[:, 0, :], stg2[:, 1, :]]
                bSt = [A.buf("st0"), A.buf("st1")]
                xt2 = A.tile([2, 8, 256], F32)
                bX2 = [A.buf("x0"), A.buf("x1")]
                hT2 = A.tile([2, 8, 256], BF16)
                bH2 = [A.buf("hT0"), A.buf("hT1")]
                sqp = A.tile([8, 256], BF16)
                bSqp = A.buf("sqp")
                sqo = A.tile([8, 256], BF16)
                bSqo = A.buf("sqo")
                rs2 = A.tile([2, 256], F32)
                bRs2 = [A.buf("rs0"), A.buf("rs1")]
                rso = A.tile([256], F32)
                bRso = A.buf("rso")
                tm1 = A.tile([256], F32)
                bT1 = A.buf("tm1")
                tm2 = A.tile([256], F32)
                bT2 = A.buf("tm2")
                rl = A.tile([2, 256], F32)
                bRl = [A.buf("rl0"), A.buf("rl1")]
                u2 = stg2.rearrange("p a b -> p (a b)").bitcast(BF16).rearrange("p (a b) -> p a b", a=32)
                bU2 = A.buf("u2")
                yv = A.tile([8, 256], F32)
                bY = A.buf("y")
                cnt = [0]
                for kc in range(8):
                    load_weight(wup[:, kc, :], w_up[l, kc * 128:(kc + 1) * 128, :], DFF, PR(16 + kc, 17 + kc), bWup, stage, bSt, cnt)
                for fc in range(32):
                    load_weight(wdn[:, fc, :], w_down[l, fc * 128:(fc + 1) * 128, :], D, None, bWdn, stage, bSt, cnt)
                gen = [0]

                def ffn_pro(i):
                    t0 = i * 256
                    xs = i % 2
                    xt = xt2[:, xs]
                    DMA(xt, fsrc.rearrange("(k p) t -> p k t", p=128)[:, :, t0:t0 + 256], [dbuf(fsrcn, i)], [bX2[xs]], key="xld%d" % xs)
                    ACT(sqp.rearrange("p a b -> p (a b)"), xt.rearrange("p a b -> p (a b)"), AF.Square, [bX2[xs]], [bSqp])
                    for kc in range(8):
                        MM(pst[6][:, 0:256], ONESB, sqp[:, kc, :], kc == 0, kc == 7, [bSqp, bC], [pb[6]])
                    norm_rstd(pst[6][:, 0:256], pb[6], 128, 1.0 / D, rs2[:, xs, :], bRs2[xs], tm1, bT1)
                    TT("vector", hT2[:, xs, 0:6, :], xt[:, 0:6, :], rs2[:, xs, :].unsqueeze(1).broadcast_to([128, 6, 256]), ALU.mult, [bX2[xs], bRs2[xs]], [bH2[xs]])
                    TT("gpsimd", hT2[:, xs, 6:8, :], xt[:, 6:8, :], rs2[:, xs, :].unsqueeze(1).broadcast_to([128, 2, 256]), ALU.mult, [bX2[xs], bRs2[xs]], [bH2[xs]])

                def ffn_post(i):
                    nonlocal_keys = final_keys
                    t0 = i * 256
                    xs = i % 2
                    xt = xt2[:, xs]
                    for kc in range(8):
                        MM(pst[7][:, 0:256], ONESB, sqo[:, kc, :], kc == 0, kc == 7, [bSqo, bC], [pb[7]])
                    norm_rstd(pst[7][:, 0:256], pb[7], 128, 1.0 / D, rso, bRso, tm2, bT2)
                    for oc in range(8):
                        STT(yv[:, oc, :], yv[:, oc, :], PR(24 + oc, 25 + oc), rso, ALU.mult, ALU.mult, [bY, bRso, bC], [bY])
                    TT("vector", yv.rearrange("p a b -> p (a b)"), yv.rearrange("p a b -> p (a b)"), xt.rearrange("p a b -> p (a b)"), ALU.add, [bY, bX2[xs]], [bY])
                    key = DMA(dst_f.rearrange("(k p) t -> p k t", p=128)[:, :, t0:t0 + 256], yv, [bY], [dbuf(dstn_f, i)], key="xst_%s" % dstn_f)
                    if key not in nonlocal_keys:
                        nonlocal_keys.append(key)

                ffn_pro(0)
                for i in range(NF):
                    xs = i % 2
                    for fc in range(32):
                        if fc == 6 and i > 0:
                            ffn_post(i - 1)
                        k = gen[0] % 4
                        gen[0] += 1
                        pt = pst[k][:, 0:256]
                        pbuf = pb[k]
                        for kc in range(8):
                            MM(pt, wup[:, kc, fc * 128:(fc + 1) * 128], hT2[:, xs, kc, :], kc == 0, kc == 7, [bWup, bH2[xs]], [pbuf])
                        r2 = fc % 2
                        if fc % 4 == 3:
                            TS("vector", rl[:, r2, :], pt, 0.0, None, ALU.max, None, [pbuf], [bRl[r2]])
                        else:
                            ACT(rl[:, r2, :], pt, AF.Relu, [pbuf], [bRl[r2]])
                        TT("gpsimd", u2[:, fc, :], rl[:, r2, :], rl[:, r2, :], ALU.mult, [bRl[r2]], [bU2] + (bSt if i == 0 else []))
                    if i + 1 < NF:
                        ffn_pro(i + 1)
                    for oc in range(8):
                        k = 4 + oc % 2
                        pt, pbuf = pst[k][:, 0:256], pb[k]
                        for fc in range(32):
                            MM(pt, wdn[:, fc, oc * 128:(oc + 1) * 128], u2[:, fc, :], fc == 0, fc == 31, [bWdn, bU2], [pbuf])
                        CP("vector", yv[:, oc, :], pt, [pbuf], [bY])
                    ACT(sqo.rearrange("p a b -> p (a b)"), yv.rearrange("p a b -> p (a b)"), AF.Square, [bY], [bSqo])
                ffn_post(NF - 1)
        P.emit(final_waits=[("dma", k) for k in final_keys])
    return nc, P, A


_CACHE = {}


def kernel(**inputs):
    x = np.asarray(inputs['x'], np.float32)
    B, S, _ = x.shape
    L = inputs['w_in'].shape[0]
    perm = _perm_in()
    cstv, tabs = _consts(S)
    w_in = np.ascontiguousarray(np.asarray(inputs['w_in'], np.float32)[:, :, perm])
    shared = {
        "w_in": w_in,
        "w_out": np.ascontiguousarray(np.asarray(inputs['w_out'], np.float32)),
        "w_up": np.ascontiguousarray(np.asarray(inputs['w_up'], np.float32)),
        "w_down": np.ascontiguousarray(np.asarray(inputs['w_down'], np.float32)),
        "par": np.stack([_params(inputs, l) for l in range(L)]),
        "cst": cstv, "tab": tabs,
    }
    in_maps = []
    for b in range(B):
        m = dict(shared)
        m["xT"] = np.ascontiguousarray(x[b].T)
        in_maps.append(m)
    nc, P, A = build(S, L)
    res = run_bass_kernel_spmd(nc, in_maps, core_ids=list(range(B)))
    out = np.stack([np.ascontiguousarray(r["yT"].T) for r in res.results], axis=0)
    return out.astype(np.float32)
```
